# Optimizing a Trainium2 kernel written in Bass

```python
import jax, jax.numpy as jnp
from jax import lax
import numpy as np

D_MODEL = 2048
BATCH = 8
SEQ = 2048
DEPTH = 1

HEAD_DIM = 64
MIX_WIDTH = D_MODEL
N_HEADS_A = (MIX_WIDTH // 2) // HEAD_DIM
N_HEADS_B = (MIX_WIDTH // 2) // HEAD_DIM
WIDTH_A = N_HEADS_A * HEAD_DIM
WIDTH_B = N_HEADS_B * HEAD_DIM
IN_WIDTH = 4 * WIDTH_A + 4 * WIDTH_B
DILATED_CONFIGS = ((128, 1), (512, 4), (2048, 16))
NUM_BUCKETS = 32
T5_MAX_DISTANCE = 1024
GRID_W = 64
NA_ROWS = 8
NA_COLS = 16
NA_COL_BLOCKS = GRID_W // NA_COLS
NA_KEY_COLS = 2 * NA_COLS
DEEPNORM_ALPHA = (2.0 * DEPTH) ** 0.25
DEEPNORM_BETA = (8.0 * DEPTH) ** -0.25
LN_EPS = 1e-5
NEG_INF = -1e30

kernel_name = "hybrid_dilated_neighbourhood_encoder"


def _t5_buckets(rel):
    half = NUM_BUCKETS // 2
    max_exact = half // 2
    n = np.abs(rel)
    large = max_exact + (np.log(np.maximum(n, 1) / max_exact)
                         / np.log(T5_MAX_DISTANCE / max_exact)
                         * (half - max_exact)).astype(np.int64)
    large = np.minimum(large, half - 1)
    return ((rel > 0).astype(np.int64) * half + np.where(n < max_exact, n, large)).astype(np.int32)


def _layer_norm(x, g, b):
    xf = x.astype(jnp.float32)
    mu = xf.mean(-1, keepdims=True)
    var = jnp.square(xf - mu).mean(-1, keepdims=True)
    return ((xf - mu) * lax.rsqrt(var + LN_EPS) * g.astype(jnp.float32)
            + b.astype(jnp.float32)).astype(x.dtype)


def _dilated_branch(q, k, v, t5_table, window, dilation):
    B, H, S, hd = q.shape
    blk = window // (2 * dilation)
    L = S // dilation
    nb = -(-L // blk)
    Lp = nb * blk
    scale = HEAD_DIM ** -0.5

    def strided(a):
        return a.reshape(B, H, L, dilation, hd).transpose(0, 1, 3, 2, 4)

    qs = jnp.pad(strided(q), ((0, 0), (0, 0), (0, 0), (0, Lp - L), (0, 0)))
    qs = qs.reshape(B, H, dilation, nb, blk, hd)

    def key_blocks(a):
        a = jnp.pad(strided(a), ((0, 0), (0, 0), (0, 0), (blk, Lp - L + blk), (0, 0)))
        a = a.reshape(B, H, dilation, nb + 2, blk, hd)
        return jnp.concatenate([a[:, :, :, :nb], a[:, :, :, 1:nb + 1], a[:, :, :, 2:]], axis=4)

    ks, vs = key_blocks(k), key_blocks(v)
    rel = np.arange(3 * blk)[None, :] - blk - np.arange(blk)[:, None]
    band = np.abs(rel) <= blk
    key_abs = (np.arange(nb)[:, None] - 1) * blk + np.arange(3 * blk)[None, :]
    valid = (key_abs >= 0) & (key_abs < L)
    mask = band[None] & valid[:, None, :]
    bias = jnp.take(t5_table, _t5_buckets(rel * dilation), axis=0)
    bias = jnp.moveaxis(bias.astype(jnp.float32), -1, 0)

    s = jnp.einsum('bhrnqd,bhrnkd->bhrnqk', qs, ks).astype(jnp.float32) * scale
    s = jnp.where(mask, s + bias[None, :, None, None], NEG_INF)
    m = s.max(-1, keepdims=True)
    p = jnp.exp(s - m)
    den = p.sum(-1, keepdims=True)
    o = jnp.einsum('bhrnqk,bhrnkd->bhrnqd', p, vs.astype(jnp.float32)) / den
    lse = (m + jnp.log(den))[..., 0]
    o = o.reshape(B, H, dilation, Lp, hd)[:, :, :, :L].transpose(0, 1, 3, 2, 4).reshape(B, H, S, hd)
    lse = lse.reshape(B, H, dilation, Lp)[..., :L].transpose(0, 1, 3, 2).reshape(B, H, S)
    return o, lse


def _dilated_attention(q, k, v, t5_table):
    outs, lses = [], []
    for window, dilation in DILATED_CONFIGS:
        o, lse = _dilated_branch(q, k, v, t5_table, window, dilation)
        outs.append(o)
        lses.append(lse)
    wts = jax.nn.softmax(jnp.stack(lses, 0), axis=0)
    return jnp.einsum('cbhs,cbhsd->bhsd', wts, jnp.stack(outs, 0))


def _neighbourhood_attention(q, k, v, rpb):
    B, H, S, hd = q.shape
    rows = S // GRID_W
    kr = min(NA_ROWS, rows)
    scale = HEAD_DIM ** -0.5
    row_start = np.clip(np.arange(rows) - kr // 2, 0, rows - kr)
    row_idx = row_start[:, None] + np.arange(kr)
    qcol = np.arange(GRID_W).reshape(NA_COL_BLOCKS, NA_COLS)
    blk_start = np.clip(qcol[:, 0] - NA_COLS // 2, 0, GRID_W - NA_KEY_COLS)
    col_idx = blk_start[:, None] + np.arange(NA_KEY_COLS)
    q_start = np.clip(qcol - NA_COLS // 2, 0, GRID_W - NA_COLS)
    kcol = col_idx[:, None, :]
    col_mask = (kcol >= q_start[:, :, None]) & (kcol < q_start[:, :, None] + NA_COLS)
    dcol = np.clip(kcol - qcol[:, :, None], -(NA_COLS - 1), NA_COLS - 1)
    drow = row_idx - np.arange(rows)[:, None]
    ri = (drow + NA_ROWS - 1)[None, :, None, :, None]
    ci = (dcol + NA_COLS - 1)[:, None, :, None, :]
    bias = rpb.astype(jnp.float32)[:, ri, ci]
    bias = jnp.where(col_mask[:, None, :, None, :], bias, NEG_INF)
    bias = bias.transpose(1, 0, 2, 3, 4, 5).reshape(NA_COL_BLOCKS, H, rows, NA_COLS, kr * NA_KEY_COLS)

    qb = q.reshape(B, H, rows, NA_COL_BLOCKS, NA_COLS, hd).transpose(3, 0, 1, 2, 4, 5)
    kg = k.reshape(B, H, rows, GRID_W, hd)
    vg = v.reshape(B, H, rows, GRID_W, hd)

    def row_block(args):
        q_blk, cidx, bias_blk = args
        k_blk = jnp.take(kg, cidx, axis=3)[:, :, row_idx].reshape(B, H, rows, kr * NA_KEY_COLS, hd)
        v_blk = jnp.take(vg, cidx, axis=3)[:, :, row_idx].reshape(B, H, rows, kr * NA_KEY_COLS, hd)
        s = jnp.einsum('bhrqd,bhrkd->bhrqk', q_blk, k_blk).astype(jnp.float32) * scale + bias_blk[None]
        p = jax.nn.softmax(s, axis=-1)
        return jnp.einsum('bhrqk,bhrkd->bhrqd', p, v_blk.astype(jnp.float32))

    o = lax.map(row_block, (qb, jnp.asarray(col_idx, jnp.int32), bias))
    return o.transpose(1, 2, 3, 0, 4, 5).reshape(B, H, S, hd)


def setup_inputs(seed: int = 0) -> dict:
    key = jax.random.key(seed)
    ks = jax.random.split(key, 7)
    x = jax.random.normal(ks[0], (BATCH, SEQ, D_MODEL), jnp.float32)
    col_scale = np.ones((IN_WIDTH,), np.float32)
    col_scale[2 * WIDTH_A:3 * WIDTH_A] = DEEPNORM_BETA
    col_scale[4 * WIDTH_A + 2 * WIDTH_B:4 * WIDTH_A + 3 * WIDTH_B] = DEEPNORM_BETA
    w_in = (jax.random.normal(ks[1], (DEPTH, D_MODEL, IN_WIDTH), jnp.float32)
            * D_MODEL ** -0.5 * jnp.asarray(col_scale))
    w_out = jax.random.normal(ks[2], (DEPTH, MIX_WIDTH, D_MODEL), jnp.float32) * (MIX_WIDTH ** -0.5 * DEEPNORM_BETA)
    t5_bias = jax.random.normal(ks[3], (NUM_BUCKETS, N_HEADS_A), jnp.float32) * 0.5
    na_rpb = jax.random.normal(ks[4], (DEPTH, N_HEADS_B, 2 * NA_ROWS - 1, 2 * NA_COLS - 1), jnp.float32) * 0.5
    ln_gain = 1.0 + 0.05 * jax.random.normal(ks[5], (DEPTH, D_MODEL), jnp.float32)
    ln_bias = 0.02 * jax.random.normal(ks[6], (DEPTH, D_MODEL), jnp.float32)
    return {"x": x, "w_in": w_in, "w_out": w_out, "t5_bias": t5_bias,
            "na_rpb": na_rpb, "ln_gain": ln_gain, "ln_bias": ln_bias}


def reference(x, w_in, w_out, t5_bias, na_rpb, ln_gain, ln_bias):
    B, S, _ = x.shape
    splits = np.cumsum([WIDTH_A] * 4 + [WIDTH_B] * 3)

    def heads(a, n):
        return a.reshape(B, S, n, HEAD_DIM).transpose(0, 2, 1, 3)

    def merge(o):
        return o.transpose(0, 2, 1, 3).reshape(B, S, -1)

    for layer in range(DEPTH):
        h = jnp.einsum('bsd,de->bse', x, w_in[layer])
        qa, ka, va, ga, qb, kb, vb, gb = jnp.split(h, splits, axis=-1)
        ya = _dilated_attention(heads(qa, N_HEADS_A), heads(ka, N_HEADS_A), heads(va, N_HEADS_A), t5_bias)
        yb = _neighbourhood_attention(heads(qb, N_HEADS_B), heads(kb, N_HEADS_B), heads(vb, N_HEADS_B), na_rpb[layer])
        ya = merge(ya) * jax.nn.silu(ga.astype(jnp.float32))
        yb = merge(yb) * jax.nn.silu(gb.astype(jnp.float32))
        y = jnp.concatenate([ya, yb], axis=-1).astype(x.dtype)
        out = jnp.einsum('bse,ed->bsd', y, w_out[layer])
        x = _layer_norm(DEEPNORM_ALPHA * x + out, ln_gain[layer], ln_bias[layer])
    return x
```

```python
import numpy as np
import concourse.bass as bass
import concourse.mybir as mybir
from concourse.bass_utils import run_bass_kernel_spmd

F32 = mybir.dt.float32
BF16 = mybir.dt.bfloat16
AF = mybir.ActivationFunctionType
ALU = mybir.AluOpType

S = 2048
D = 2048
NEG = -30000.0
SCALE = 0.125
ALPHA = 2.0 ** 0.25
LN_EPS = 1e-5
NTA = 7
NTB = 9

ENGS = ("pe", "act", "dve", "pool", "sp")


class Sched:
    def __init__(self):
        self.prog = {e: [] for e in ENGS}
        self.lw = {}
        self.lr = {}
        self.dma_cnt = {}

    def _deps(self, reads, writes):
        deps = []
        for k in reads:
            if k in self.lw:
                deps.append(self.lw[k])
        for k in writes:
            if k in self.lw:
                deps.append(self.lw[k])
            for p in self.lr.get(k, {}).values():
                deps.append(p)
        return deps

    def op(self, eng, fn, reads=(), writes=()):
        deps = self._deps(reads, writes)
        idx = len(self.prog[eng])
        self.prog[eng].append(dict(fn=fn, deps=deps, kind="op"))
        me = ("eng", eng, idx)
        for k in reads:
            self.lr.setdefault(k, {})[eng] = me
        for k in writes:
            self.lw[k] = me
            self.lr[k] = {}
        return me

    def dma(self, queue, fn, reads=(), writes=(), sem=None):
        deps = self._deps(reads, writes)
        cnt = self.dma_cnt.get(sem, 0) + 1
        self.dma_cnt[sem] = cnt
        self.prog[queue].append(dict(fn=fn, deps=deps, kind="dma", sem=sem))
        me = ("dma", sem, cnt)
        for k in reads:
            self.lr.setdefault(k, {})["dma:" + sem] = me
        for k in writes:
            self.lw[k] = me
            self.lr[k] = {}
        return me

    def barrier(self):
        deps = []
        for e in ENGS:
            for idx in range(len(self.prog[e]) - 1, -1, -1):
                r = self.prog[e][idx]
                if r["kind"] == "op" and r["fn"] is not None:
                    deps.append(("eng", e, idx))
                    break
        for s, c in self.dma_cnt.items():
            deps.append(("dma", s, c))
        for e in ENGS:
            self.prog[e].append(dict(fn=None, deps=list(deps), kind="wait"))

    def emit(self, nc):
        need = {e: set() for e in ENGS}
        for e in ENGS:
            for rec in self.prog[e]:
                for d in rec["deps"]:
                    if d[0] == "eng":
                        assert self.prog[d[1]][d[2]]["kind"] == "op", (d, self.prog[d[1]][d[2]])
                        need[d[1]].add(d[2])
        val = {e: {idx: r + 1 for r, idx in enumerate(sorted(need[e]))} for e in ENGS}
        sems = {e: nc.alloc_semaphore("s_" + e) for e in ENGS}
        dsems = {s: nc.alloc_semaphore("d_%d" % i) for i, s in enumerate(self.dma_cnt)}
        prog = self.prog

        def run(e, eng):
            seen = {}
            for idx, rec in enumerate(prog[e]):
                for d in rec["deps"]:
                    if d[0] == "eng":
                        if d[1] == e and e == "pe":
                            continue
                        key = ("e", d[1])
                        s = sems[d[1]]
                        v = val[d[1]][d[2]]
                    else:
                        key = ("d", d[1])
                        s = dsems[d[1]]
                        v = 16 * d[2]
                    if seen.get(key, 0) >= v:
                        continue
                    eng.wait_ge(s, v)
                    seen[key] = v
                if rec["fn"] is None:
                    continue
                ins = rec["fn"](eng)
                if rec["kind"] == "dma":
                    ins.then_inc(dsems[rec["sem"]], 16)
                elif idx in val[e]:
                    ins.then_inc(sems[e], 1)

        with nc.Block() as blk:
            @blk.tensor
            def _(eng):
                run("pe", eng)

            @blk.scalar
            def _(eng):
                run("act", eng)

            @blk.vector
            def _(eng):
                run("dve", eng)

            @blk.gpsimd
            def _(eng):
                run("pool", eng)

            @blk.sync
            def _(eng):
                run("sp", eng)


def blocks_A():
    groups = []
    for t in range(16):
        blks = []
        if t == 0:
            blks.append(dict(k=(0, 1, 64), bt=5, vt=0))
        else:
            blks.append(dict(k=(128 * t - 64, 1, 128), bt=0, vt=t))
        if t == 15:
            blks.append(dict(k=(128 * t + 64, 1, 64), bt=1, vt=16))
        else:
            blks.append(dict(k=(128 * t + 64, 1, 128), bt=1, vt=t + 1))
        groups.append(dict(branch=1, obank=t // 4, ocol=128 * (t % 4), q=(128 * t, 1), blocks=blks))
    for c in range(4):
        for t in range(4):
            blks = []
            if t == 0:
                blks.append(dict(k=(c, 4, 64), bt=6, vt=17 + 5 * c))
            else:
                blks.append(dict(k=(c + 4 * (128 * t - 64), 4, 128), bt=2, vt=17 + 5 * c + t))
            if t == 3:
                blks.append(dict(k=(c + 4 * (128 * t + 64), 4, 64), bt=3, vt=17 + 5 * c + 4))
            else:
                blks.append(dict(k=(c + 4 * (128 * t + 64), 4, 128), bt=3, vt=17 + 5 * c + t + 1))
            groups.append(dict(branch=2, obank=c, ocol=128 * t, q=(c + 4 * 128 * t, 4), blocks=blks))
    for r in range(16):
        groups.append(dict(branch=3, obank=r // 4, ocol=128 * (r % 4), q=(r, 16),
                           blocks=[dict(k=(r, 16, 128), bt=4, vt=37 + r)]))
    return groups


def vtiles_A():
    tiles = []
    for j in range(17):
        if j == 0:
            tiles.append((0, (0, 1, 64)))
        elif j == 16:
            tiles.append((16, (1984, 1, 64)))
        else:
            tiles.append((j, (128 * j - 64, 1, 128)))
    for c in range(4):
        for j in range(5):
            if j == 0:
                tiles.append((17 + 5 * c, (c, 4, 64)))
            elif j == 4:
                tiles.append((17 + 5 * c + 4, (c + 4 * 448, 4, 64)))
            else:
                tiles.append((17 + 5 * c + j, (c + 4 * (128 * j - 64), 4, 128)))
    for r in range(16):
        tiles.append((37 + r, (r, 16, 128)))
    return tiles


def _rs(rho):
    return min(max(rho - 4, 0), 24)


def blocks_B():
    groups = []
    for i in range(16):
        blks = []
        for kt in range(16):
            pat = []
            for a in (0, 1):
                for b in (0, 1):
                    kr, qr = 2 * kt + a, 2 * i + b
                    pat.append(_rs(qr) <= kr <= _rs(qr) + 7)
            if not any(pat):
                continue
            dl = kt - i
            interior = [(-4 <= 2 * dl + a - b <= 3) for a in (0, 1) for b in (0, 1)]
            if pat == interior and -2 <= dl <= 2:
                bt = dl + 2
            elif all(pat) and dl == 2:
                bt = 5
            elif all(pat) and dl == 3:
                bt = 6
            elif all(pat) and dl == -3:
                bt = 7
            elif all(pat) and dl == -2:
                bt = 8
            else:
                raise AssertionError((i, kt, pat))
            blks.append(dict(k=(128 * kt, 1, 128), bt=bt, vt=kt))
        groups.append(dict(branch=0, obank=i // 4, ocol=128 * (i % 4), q=(128 * i, 1), blocks=blks))
    return groups


def vtiles_B():
    return [(j, (128 * j, 1, 128)) for j in range(16)]


def _t5_buckets(rel):
    nb, maxd = 32, 1024
    half = nb // 2
    max_exact = half // 2
    n = np.abs(rel)
    large = max_exact + (np.log(np.maximum(n, 1) / max_exact)
                         / np.log(maxd / max_exact) * (half - max_exact)).astype(np.int64)
    large = np.minimum(large, half - 1)
    return ((rel > 0).astype(np.int64) * half + np.where(n < max_exact, n, large)).astype(np.int64)


def host_biasA(t5):
    k = np.arange(128)[:, None]
    q = np.arange(128)[None, :]
    specs = []
    for d in (1, 4):
        specs.append((k - 64 - q, (k >= q), d))
        specs.append((k + 64 - q, (k <= q), d))
    specs.append((k - q, (np.abs(k - q) <= 64), 16))
    for d in (1, 4):
        specs.append((k - q, ((k - q) >= -64) & (k < 64), d))
    out = np.full((16, 128, NTA, 128), NEG, np.float32)
    for ti, (rel, valid, d) in enumerate(specs):
        bk = _t5_buckets(rel * d)
        vals = t5[bk]
        for h in range(16):
            out[h, :, ti, :] = np.where(valid, vals[:, :, h], np.float32(NEG))
    out = out.reshape(8, 2, 128, NTA, 128).transpose(0, 2, 1, 3, 4).reshape(8, 128, 2 * NTA * 128)
    return np.ascontiguousarray(out)


def host_biasB(rpb):
    a = (np.arange(128) // 64)[:, None]
    cp = (np.arange(128) % 64)[:, None]
    b = (np.arange(128) // 64)[None, :]
    c = (np.arange(128) % 64)[None, :]
    qs = np.clip(c - 8, 0, 48)
    colv = (cp >= qs) & (cp < qs + 16)
    dcol = np.clip(cp - c, -15, 15) + 15
    variants = [(-2, "int"), (-1, "int"), (0, "int"), (1, "int"), (2, "int"),
                (2, "all"), (3, "all"), (-3, "all"), (-2, "all")]
    out = np.full((16, 128, NTB, 128), NEG, np.float32)
    for ti, (dl, kind) in enumerate(variants):
        dr = 2 * dl + a - b
        rowv = ((dr >= -4) & (dr <= 3)) if kind == "int" else np.ones_like(dr, bool)
        ri = np.clip(dr + 7, 0, 14)
        valid = rowv & colv
        for h in range(16):
            out[h, :, ti, :] = np.where(valid, rpb[h][ri, dcol], np.float32(NEG))
    out = out.reshape(8, 2, 128, NTB, 128).transpose(0, 2, 1, 3, 4).reshape(8, 128, 2 * NTB * 128)
    return np.ascontiguousarray(out)


def host_w_in(w_in):
    w = w_in.reshape(16, 128, 2, 4, 8, 128)
    w = w.transpose(2, 4, 1, 0, 3, 5)
    return np.ascontiguousarray(w.reshape(16, 128, 16 * 512))


def host_w_out(w_out):
    w = w_out.reshape(16, 128, 2048).transpose(1, 0, 2)
    return np.ascontiguousarray(w.reshape(128, 16 * 2048))


def host_consts():
    c = np.zeros((128, 128 + 192 + 64), np.float32)
    c[:, 320:384] = 1.0
    c[np.arange(128), np.arange(128)] = 1.0
    c[np.arange(64), 128 + 64 + np.arange(64)] = 1.0
    return c


def build(pairs=tuple(range(16)), debug_y=False):
    npairs = len(pairs)
    nc = bass.Bass("TRN2", target_bir_lowering=False)
    x = nc.dram_tensor("x", [S, D], F32, kind="ExternalInput").ap()
    w_in = nc.dram_tensor("w_in", [16, 128, 8192], F32, kind="ExternalInput").ap()
    w_out = nc.dram_tensor("w_out", [128, 16 * 2048], F32, kind="ExternalInput").ap()
    biasA = nc.dram_tensor("biasA", [8, 128, 2 * NTA * 128], F32, kind="ExternalInput").ap()
    biasB = nc.dram_tensor("biasB", [8, 128, 2 * NTB * 128], F32, kind="ExternalInput").ap()
    lng = nc.dram_tensor("lng", [128, D], F32, kind="ExternalInput").ap()
    lnb = nc.dram_tensor("lnb", [128, D], F32, kind="ExternalInput").ap()
    consts = nc.dram_tensor("consts", [128, 384], F32, kind="ExternalInput").ap()
    out = nc.dram_tensor("out", [S, D], F32, kind="ExternalOutput").ap()
    yscr_t = nc.dram_tensor("yscr", [16, 128, 16, 128], BF16,
                            kind=("ExternalOutput" if debug_y else "Internal"))
    yscr = yscr_t.ap()

    sb = nc.alloc_sbuf_tensor

    def alias(name, shape, dtype, base, off=0):
        return nc.alloc_sbuf_tensor_at(name, shape, dtype, offset=nc.lookup_mloc(base).addr + off)

    BIG = sb("BIG", [128, 16, 2048], BF16)
    W = [sb("W%d" % i, [128, 8192], BF16) for i in range(2)]
    ACC = [sb("ACC%d" % i, [128, 2048], F32) for i in range(2)]
    G = [sb("G%d" % i, [128, 2048], BF16) for i in range(2)]
    QT = [sb("QT%d" % i, [128, 2048], BF16) for i in range(2)]
    KT = [sb("KT%d" % i, [128, 2048], BF16) for i in range(2)]
    VT = [sb("VT%d" % i, [128, 2048], BF16) for i in range(2)]
    Vt = [sb("Vt%d" % i, [128, 53, 2, 65], BF16) for i in range(2)]
    biasT = [sb("biasT%d" % i, [128, 2 * NTB * 128], F32) for i in range(2)]
    Sb = [sb("Sb%d" % i, [128, 512], F32) for i in range(2)]
    PT = [sb("PT%d" % i, [128, 512], BF16) for i in range(5)]
    yn = [sb("yn%d" % i, [128, 512], BF16) for i in range(2)]
    ybuf = [sb("ybuf%d" % i, [128, 512], BF16) for i in range(2)]
    cst = sb("cst", [128, 384], BF16)
    hl = sb("hl", [128, 2, 2, 512], BF16)
    ones32 = sb("ones32", [128, 64], F32)
    stt = [sb("stt%d" % i, [128, 24], F32) for i in range(2)]
    mv = [sb("mv%d" % i, [128, 4], F32) for i in range(2)]
    xin = [alias("xin0", [128, 2048], BF16, ACC[0]), alias("xin1", [128, 2048], BF16, ACC[0], 4096)]
    xres = [alias("xres0", [128, 2048], F32, QT[0]), alias("xres1", [128, 2048], F32, G[0])]
    rbuf = [alias("rbuf0", [128, 2048], F32, KT[0]), alias("rbuf1", [128, 2048], F32, VT[0])]
    yTt = [alias("yTt0", [128, 16, 128], BF16, Vt[0]), alias("yTt1", [128, 16, 128], BF16, Vt[0], 4096)]

    ad = lambda t: nc.lookup_mloc(t).addr
    assert ad(G[1]) == ad(G[0]) + 4096 and ad(QT[1]) == ad(QT[0]) + 4096 and ad(KT[1]) == ad(KT[0]) + 4096 and ad(VT[1]) == ad(VT[0]) + 4096
    P = [nc.alloc_psum_tensor("P%d" % i, [128, 512], F32) for i in range(2)]
    SP = [nc.alloc_psum_tensor("S%d" % i, [128, 512], F32) for i in range(4)]
    OP = [nc.alloc_psum_tensor("O%d" % i, [128, 512], F32) for i in range(2)]
    TP = [P[i][:, :].bitcast(BF16) for i in range(2)]

    ident = cst[:, 0:128]
    E_lo = cst[0:64, 192:320]
    E_sh = cst[0:64, 128:256]

    def AP(t, off, dims):
        return bass.AP(t, off, [list(d) for d in dims])

    def pstride(t):
        return t[:].ap[0][0]

    def col_ap(t, prow, nrow, start, step, n):
        return AP(t, prow * pstride(t) + start, [[pstride(t), nrow], [step, n]])

    sc = Sched()
    cnt = dict(p=0, s=0, o=0, t=0, sb=0, pt=0, yb=0)

    def nxt(k, n):
        v = cnt[k] % n
        cnt[k] += 1
        return v

    sc.dma("pool", lambda e: e.dma_start(out=cst[:], in_=consts), writes=["cst"], sem="cst")
    sc.op("dve", lambda e: e.memset(ones32[:], 1.0), writes=["ones32"])
    for b in range(2):
        sc.op("dve", lambda e, b=b: e.memset(Vt[b][:, :, :, 64:65], 1.0), writes=[("Vt", b)])

    def load_w(p, sl):
        for c4 in range(4):
            sc.dma("pool", lambda e, sl=sl, p=p, c4=c4: e.dma_start(
                out=W[sl][:, 2048 * c4:2048 * c4 + 2048], in_=w_in[p, :, 2048 * c4:2048 * c4 + 2048]),
                writes=[("W", sl)], sem="W%d" % sl)

    grpsA = blocks_A()
    grpsB = blocks_B()
    vtA = vtiles_A()
    vtB = vtiles_B()

    def proj_group(pi, oi, ck):
        p = pairs[pi]
        wsl = pi % 2
        bsl = pi % 2
        pb = nxt("p", 2)
        for dc in range(16):
            sc.op("pe", lambda e, pb=pb, wsl=wsl, dc=dc, oi=oi, ck=ck: e.matmul(
                P[pb][:, :], W[wsl][:, 512 * dc + 128 * oi:512 * dc + 128 * oi + 128],
                BIG[:, dc, 512 * ck:512 * ck + 512], start=(dc == 0), stop=(dc == 15)),
                reads=[("W", wsl), ("xT", 0, ck), ("xT", 1, ck)], writes=[("P", pb)])
        cs = slice(512 * ck, 512 * ck + 512)
        if oi == 0:
            sc.op("act", lambda e, pb=pb, cs=cs: e.copy(QT[bsl][:, cs], P[pb][:, :]),
                  reads=[("P", pb)], writes=[("QT", bsl, ck)])
        elif oi == 1:
            sc.op("act", lambda e, pb=pb, cs=cs: e.copy(KT[bsl][:, cs], P[pb][:, :]),
                  reads=[("P", pb)], writes=[("KT", bsl, ck)])
        elif oi == 2:
            sc.op("dve", lambda e, pb=pb, cs=cs: e.tensor_copy(VT[bsl][:, cs], P[pb][:, :]),
                  reads=[("P", pb)], writes=[("VT", bsl, ck)])
        else:
            sc.op("act", lambda e, pb=pb, cs=cs: e.activation(G[bsl][:, cs], P[pb][:, :], AF.Silu),
                  reads=[("P", pb)], writes=[("G", bsl, ck)])

    def units_P(pi):
        p = pairs[pi]
        isA = p < 8
        bsl = pi % 2
        units = []

        def u_first():
            if isA:
                sc.dma("sp", lambda e: e.dma_start(out=biasT[bsl][:, 0:2 * NTA * 128], in_=biasA[p]),
                       writes=[("bias", bsl)], sem="bias%d" % bsl)
            else:
                sc.dma("sp", lambda e: e.dma_start(out=biasT[bsl][:, 0:2 * NTB * 128], in_=biasB[p - 8]),
                       writes=[("bias", bsl)], sem="bias%d" % bsl)
        units.append(u_first)
        if pi == 0:
            pass
        else:
            for oi in (2, 1, 0, 3):
                for ck in range(4):
                    units.append(lambda oi=oi, ck=ck: proj_group(pi, oi, ck))
        vts = vtA if isA else vtB
        full = [v for v in vts if v[1][2] == 128]
        edge = [v for v in vts if v[1][2] == 64]
        VTk = [("VT", bsl, c) for c in range(4)]
        i0 = 0
        runs = []
        while i0 < len(full):
            run = [full[i0]]
            while len(run) < 8 and i0 + len(run) < len(full) and full[i0 + len(run)][0] == run[-1][0] + 1:
                run.append(full[i0 + len(run)])
            i0 += len(run)
            runs.append(run)

        def u_run(run):
            tb = nxt("p", 2)
            for j, (ti, (st, sp_, nk)) in enumerate(run):
                sc.op("pe", lambda e, tb=tb, j=j, st=st, sp_=sp_: e.transpose(
                    TP[tb][:, 128 * j:128 * j + 128], col_ap(VT[bsl], 0, 128, st, sp_, 128), ident),
                    reads=VTk + ["cst"], writes=[("P", tb)])
            n = len(run)
            t0 = run[0][0]
            dst = Vt[bsl][:, t0:t0 + n, :, 0:64]
            src = TP[tb][:, 0:128 * n].rearrange("p (a h c) -> p a h c", a=n, h=2)
            sc.op("act", lambda e, dst=dst, src=src: e.copy(dst, src),
                  reads=[("P", tb)], writes=[("Vt", bsl)])

        def u_edges(es):
            tb = nxt("p", 2)
            for j, (ti, (st, sp_, nk)) in enumerate(es):
                sc.op("pe", lambda e, tb=tb, j=j, st=st, sp_=sp_: e.transpose(
                    TP[tb][0:64, 128 * j:128 * j + 128], col_ap(VT[bsl], 0, 128, st, sp_, 64), ident),
                    reads=VTk + ["cst"], writes=[("P", tb)])
            for j, (ti, _) in enumerate(es):
                dst = Vt[bsl][0:64, ti, :, 0:64]
                src = TP[tb][0:64, 128 * j:128 * j + 128].rearrange("p (h c) -> p h c", h=2)
                sc.op("act", lambda e, dst=dst, src=src: e.copy(dst, src),
                      reads=[("P", tb)], writes=[("Vt", bsl)])
        for run in runs:
            units.append(lambda run=run: u_run(run))
        for i0 in range(0, len(edge), 8):
            units.append(lambda es=edge[i0:i0 + 8]: u_edges(es))
        if pi + 1 < npairs:
            units.insert(1, lambda: load_w(pairs[pi + 1], (pi + 1) % 2))
        return units

    def units_G(pi):
        return [lambda ck=ck: proj_group(pi, 3, ck) for ck in range(4)]

    def units_T(pi):
        p = pairs[pi]
        isA = p < 8
        bsl = pi % 2
        NT = NTA if isA else NTB
        grps = grpsA if isA else grpsB
        QTk = [("QT", bsl, c) for c in range(4)]
        KTk = [("KT", bsl, c) for c in range(4)]
        units = []
        state = {}
        allA, allB = [], []
        for e_ in range(2):
            prow = 64 * e_
            acc = ACC[e_]
            ps_acc = pstride(acc)
            order = sorted(range(len(grps)), key=lambda gi: (grps[gi]["branch"], grps[gi]["obank"], grps[gi]["ocol"]))
            stream = []
            for gi in order:
                g = grps[gi]
                nb = len(g["blocks"])
                for bi, b in enumerate(g["blocks"]):
                    stream.append(dict(g=g, b=b, first=(bi == 0), last=(bi == nb - 1)))
            packs = []
            cur = []
            for it in stream:
                if it["b"]["k"][2] == 64:
                    if cur:
                        packs.append(cur)
                        cur = []
                    packs.append([it])
                else:
                    cur.append(it)
                    if len(cur) == 4:
                        packs.append(cur)
                        cur = []
            if cur:
                packs.append(cur)

            def flush_obank(key, ob, e_=e_, acc=acc, ps_acc=ps_acc):
                br, obk = key
                src = OP[ob][0:65, :]
                if br in (0, 1):
                    dst = acc[0:65, 512 * obk:512 * obk + 512]
                    sc.op("act", lambda e, dst=dst, src=src: e.copy(dst, src),
                          reads=[("O", ob)], writes=[("ACC", e_)])
                else:
                    if br == 2:
                        view = AP(acc, obk, [[ps_acc, 65], [512, 4], [4, 128]])
                    else:
                        view = AP(acc, 4 * obk, [[ps_acc, 65], [1, 4], [16, 128]])
                    src3 = src.rearrange("p (a b) -> p a b", a=4)
                    sc.op("dve", lambda e, view=view, src3=src3: e.tensor_tensor(view, src3, view, ALU.add),
                          reads=[("O", ob), ("ACC", e_)], writes=[("ACC", e_)])

            def u_packA(pk, e_=e_, prow=prow):
                nk = pk[0]["b"]["k"][2]
                nblk = len(pk)
                sbk = nxt("s", 4)
                for j, it in enumerate(pk):
                    ks, kp, _ = it["b"]["k"]
                    qs, qp = it["g"]["q"]
                    sc.op("pe", lambda e, sbk=sbk, j=j, ks=ks, kp=kp, qs=qs, qp=qp: e.matmul(
                        SP[sbk][0:nk, 128 * j:128 * j + 128],
                        col_ap(KT[bsl], prow, 64, ks, kp, nk), col_ap(QT[bsl], prow, 64, qs, qp, 128),
                        start=True, stop=True),
                        reads=QTk + KTk, writes=[("S", sbk)])
                sbs = nxt("sb", 2)
                bts = [it["b"]["bt"] for it in pk]
                j0 = 0
                while j0 < nblk:
                    j1 = j0 + 1
                    while j1 < nblk and bts[j1] == bts[j1 - 1] + 1:
                        j1 += 1
                    boff = (e_ * NT + bts[j0]) * 128
                    n = 128 * (j1 - j0)
                    sc.op("dve", lambda e, sbs=sbs, sbk=sbk, j0=j0, n=n, boff=boff: e.scalar_tensor_tensor(
                        Sb[sbs][0:nk, 128 * j0:128 * j0 + n], SP[sbk][0:nk, 128 * j0:128 * j0 + n], SCALE,
                        biasT[bsl][0:nk, boff:boff + n], ALU.mult, ALU.add),
                        reads=[("S", sbk), ("bias", bsl)], writes=[("Sb", sbs)])
                    j0 = j1
                pts = nxt("pt", 5)
                sc.op("act", lambda e, pts=pts, sbs=sbs: e.activation(
                    PT[pts][0:nk, 0:128 * nblk], Sb[sbs][0:nk, 0:128 * nblk], AF.Exp),
                    reads=[("Sb", sbs)], writes=[("PT", pts)])
                pk[0]["pts"] = pts

            def u_packB(pk, e_=e_, flush_obank=flush_obank):
                nk = pk[0]["b"]["k"][2]
                pts = pk[0]["pts"]
                for j, it in enumerate(pk):
                    g = it["g"]
                    key = (e_, g["branch"], g["obank"])
                    if key != state.get("cur"):
                        if state.get("cur") is not None:
                            state["flush"](state["cur"][1:], state["ob"])
                        state["cur"] = key
                        state["ob"] = nxt("o", 2)
                        state["flush"] = flush_obank
                    ob = state["ob"]
                    vt = it["b"]["vt"]
                    oc = g["ocol"]
                    sc.op("pe", lambda e, ob=ob, oc=oc, vt=vt, pts=pts, j=j, f=it["first"], l=it["last"]: e.matmul(
                        OP[ob][0:65, oc:oc + 128], Vt[bsl][0:nk, vt, e_, :], PT[pts][0:nk, 128 * j:128 * j + 128],
                        start=f, stop=l),
                        reads=[("Vt", bsl), ("PT", pts)], writes=[("O", ob)])

            for pk in packs:
                pk = [dict(it) for it in pk]
                allA.append(lambda pk=pk, f=u_packA: f(pk))
                allB.append(lambda pk=pk, f=u_packB: f(pk))

        LAG = 4
        for k in range(len(allA)):
            units.append(allA[k])
            if k >= LAG:
                units.append(allB[k - LAG])
        for k in range(max(0, len(allA) - LAG), len(allA)):
            units.append(allB[k])

        def u_flush_last():
            state["flush"](state["cur"][1:], state["ob"])
            state["cur"] = None
        units.append(u_flush_last)

        def fS1(ck):
            cs = slice(512 * ck, 512 * ck + 512)
            for e_ in range(2):
                sc.op("act", lambda e, e_=e_: e.activation(ACC[e_][64:65, cs], ACC[e_][64:65, cs], AF.Ln),
                      reads=[("ACC", e_)], writes=[("ACCd", e_, ck)])
                sc.op("act", lambda e, e_=e_: e.activation(ACC[e_][64:65, cs], ACC[e_][64:65, cs], AF.Exp, scale=-1.0),
                      reads=[("ACCd", e_, ck)], writes=[("ACCd", e_, ck)])
                sc.op("act", lambda e, e_=e_: e.copy(hl[64:65, 0, e_, :], ACC[e_][64:65, cs]),
                      reads=[("ACCd", e_, ck)], writes=[("hl", 0, e_)])
            for e_ in range(2):
                sc.op("pool", lambda e, e_=e_: e.tensor_tensor(
                    hl[64:65, 1, e_, :], ACC[e_][64:65, cs], hl[64:65, 0, e_, :], ALU.subtract),
                    reads=[("ACCd", e_, ck), ("hl", 0, e_)], writes=[("hl", 1, e_)])

        def fS2(ck):
            cs = slice(512 * ck, 512 * ck + 512)
            pbs = []
            for e_ in range(2):
                pb = nxt("p", 2)
                pbs.append(pb)
                sc.op("pe", lambda e, pb=pb, e_=e_: e.matmul(
                    P[pb][0:64, :], cst[64:65, 320:384], hl[64:65, 0, e_, :], start=True, stop=False),
                    reads=["cst", ("hl", 0, e_)], writes=[("P", pb)])
                sc.op("pe", lambda e, pb=pb, e_=e_: e.matmul(
                    P[pb][0:64, :], cst[64:65, 320:384], hl[64:65, 1, e_, :], start=False, stop=True),
                    reads=["cst", ("hl", 1, e_)], writes=[("P", pb)])
            for e_ in range(2):
                pb = pbs[e_]
                sc.op("dve", lambda e, pb=pb, e_=e_: e.tensor_tensor(
                    yn[e_][0:64, :], ACC[e_][0:64, cs], P[pb][0:64, :], ALU.mult),
                    reads=[("P", pb), ("ACC", e_)], writes=[("yn", e_)])

        def fS3(ck):
            cs = slice(512 * ck, 512 * ck + 512)
            pb = nxt("p", 2)
            sc.op("pe", lambda e, pb=pb: e.matmul(P[pb][:, :], E_lo, yn[0][0:64, :], start=True, stop=False),
                  reads=["cst", ("yn", 0)], writes=[("P", pb)])
            sc.op("pe", lambda e, pb=pb: e.matmul(P[pb][:, :], E_sh, yn[1][0:64, :], start=False, stop=True),
                  reads=["cst", ("yn", 1)], writes=[("P", pb)])
            ys = nxt("yb", 2)
            sc.op("dve", lambda e, pb=pb, ys=ys: e.tensor_tensor(
                ybuf[ys][:, :], P[pb][:, :], G[bsl][:, cs], ALU.mult),
                reads=[("P", pb), ("G", bsl, ck)], writes=[("ybuf", ys)])
            dst = yscr[4 * ck:4 * ck + 4, :, p, :].rearrange("i e t -> e i t")
            src = ybuf[ys][:, :].rearrange("e (i t) -> e i t", i=4)
            sc.dma("sp", lambda e, dst=dst, src=src: e.dma_start(out=dst, in_=src),
                   reads=[("ybuf", ys)], writes=["yscr"], sem="yst%d" % ys)

        seq = [("1", 0), ("2", 0), ("1", 1), ("3", 0), ("2", 1), ("1", 2), ("3", 1), ("2", 2), ("1", 3),
               ("3", 2), ("2", 3), ("3", 3)]
        fmap = {"1": fS1, "2": fS2, "3": fS3}
        for kind, ck in seq:
            units.append(lambda kind=kind, ck=ck: fmap[kind](ck))
        return units

    def phase0():
        for i in range(16):
            sl = i % 2
            sc.dma("pool", lambda e, sl=sl, i=i: e.dma_start(out=xin[sl][:], in_=x[128 * i:128 * i + 128, :]),
                   writes=[("xin", sl)], sem="xin%d" % sl)
            if i == 1 and npairs > 0:
                load_w(pairs[0], 0)
            for g in range(2):
                tb = nxt("p", 2)
                for j in range(8):
                    dc = 8 * g + j
                    sc.op("pe", lambda e, tb=tb, j=j, sl=sl, dc=dc: e.transpose(
                        TP[tb][:, 128 * j:128 * j + 128], xin[sl][:, 128 * dc:128 * dc + 128], ident),
                        reads=[("xin", sl), "cst"], writes=[("P", tb)])
                dst = BIG[:, 8 * g:8 * g + 8, 128 * i:128 * i + 128]
                src = TP[tb][:, :].rearrange("p (a b) -> p a b", a=8)
                if g == 0:
                    sc.op("act", lambda e, dst=dst, src=src: e.copy(dst, src),
                          reads=[("P", tb)], writes=[("xT", g, i // 4)])
                else:
                    sc.op("dve", lambda e, dst=dst, src=src: e.tensor_copy(dst, src),
                          reads=[("P", tb)], writes=[("xT", g, i // 4)])
            if i % 4 == 3 and npairs > 0:
                for oi in (2, 1, 0, 3):
                    proj_group(0, oi, i // 4)


    phase0()
    if npairs > 0:
        for u in units_P(0):
            u()
    for pi in range(npairs):
        tu = units_T(pi)
        pu = units_P(pi + 1) if pi + 1 < npairs else []
        nT, nP = len(tu), len(pu)
        span = max(1, int(nT * 0.97))
        ip = 0
        for it_, u in enumerate(tu):
            u()
            while ip < nP and (ip + 1) * span <= (it_ + 1) * nP:
                pu[ip]()
                ip += 1
        while ip < nP:
            pu[ip]()
            ip += 1
        if pi + 1 == npairs - 1 or npairs == 1:
            for c8 in range(8):
                sc.dma("pool", lambda e, c8=c8: e.dma_start(
                    out=BIG[:, 2 * c8:2 * c8 + 2, :],
                    in_=w_out[:, 4096 * c8:4096 * c8 + 4096].rearrange("p (a b) -> p a b", a=2)),
                    writes=[("WO", c8)] + [("xT", g_, c_) for g_ in range(2) for c_ in range(4)], sem="WO%d" % c8)

    sc.barrier()
    WO = BIG
    WOk = [("WO", c8) for c8 in range(8)]
    gain = ACC[0]
    lbias = ACC[1]
    sc.dma("sp", lambda e: e.dma_start(out=gain[:], in_=lng), writes=["gain"], sem="gain")
    sc.dma("sp", lambda e: e.dma_start(out=lbias[:], in_=lnb), writes=["lbias"], sem="lbias")
    def p2_loads(i):
        ysl = i % 2
        xsl = i % 2
        sc.dma("sp", lambda e, ysl=ysl, i=i: e.dma_start(out=yTt[ysl][:], in_=yscr[i]),
               reads=["yscr"], writes=[("yTt", ysl)], sem="yTt%d" % ysl)
        sc.dma("sp", lambda e, i=i, xsl=xsl: e.dma_start(out=xres[xsl][:], in_=x[128 * i:128 * i + 128, :]),
               writes=[("xres", xsl)], sem="xres%d" % xsl)

    p2_loads(0)
    pend_tail = []
    for i in range(16):
        ysl = i % 2
        rsl = i % 2
        xsl = i % 2
        if i + 1 < 16:
            p2_loads(i + 1)
        for n in range(4):
            cs = slice(512 * n, 512 * n + 512)
            pb = nxt("p", 2)
            for ec in range(16):
                sc.op("pe", lambda e, pb=pb, ysl=ysl, ec=ec, cs=cs: e.matmul(
                    P[pb][:, :], yTt[ysl][:, ec, :], WO[:, ec, cs], start=(ec == 0), stop=(ec == 15)),
                    reads=[("yTt", ysl), ("WO", ec // 2)], writes=[("P", pb)])
            sc.op("dve", lambda e, pb=pb, rsl=rsl, cs=cs, xsl=xsl: e.scalar_tensor_tensor(
                rbuf[rsl][:, cs], xres[xsl][:, cs], ALPHA, P[pb][:, :], ALU.mult, ALU.add),
                reads=[("P", pb), ("xres", xsl)], writes=[("rbuf", rsl, n)])
            sc.op("dve", lambda e, rsl=rsl, n=n, cs=cs: e.bn_stats(stt[rsl][:, 6 * n:6 * n + 6], rbuf[rsl][:, cs]),
                  reads=[("rbuf", rsl, n)], writes=[("stt", rsl, n)])
            if pend_tail:
                pend_tail.pop(0)()
                if n == 3:
                    while pend_tail:
                        pend_tail.pop(0)()
        rk = [("rbuf", rsl, n) for n in range(4)]
        sc.op("dve", lambda e, rsl=rsl: e.bn_aggr(mv[rsl][:, 0:2], stt[rsl][:, 0:24]),
              reads=[("stt", rsl, n) for n in range(4)], writes=[("mv", rsl)])
        sc.op("dve", lambda e, rsl=rsl: e.tensor_scalar(
            mv[rsl][:, 2:3], mv[rsl][:, 1:2], LN_EPS, None, ALU.add),
            reads=[("mv", rsl)], writes=[("mv2", rsl)])
        sc.op("act", lambda e, rsl=rsl: e.sqrt(mv[rsl][:, 2:3], mv[rsl][:, 2:3]),
              reads=[("mv2", rsl)], writes=[("mv2", rsl)])
        sc.op("dve", lambda e, rsl=rsl: e.reciprocal(mv[rsl][:, 2:3], mv[rsl][:, 2:3]),
              reads=[("mv2", rsl)], writes=[("mv2", rsl)])
        sc.op("dve", lambda e, rsl=rsl: e.scalar_tensor_tensor(
            mv[rsl][:, 3:4], mv[rsl][:, 0:1], -1.0, mv[rsl][:, 2:3], ALU.mult, ALU.mult),
            reads=[("mv", rsl), ("mv2", rsl)], writes=[("mv3", rsl)])
        sc.op("act", lambda e, rsl=rsl: e.activation(
            rbuf[rsl][:, :], rbuf[rsl][:, :], AF.Identity, bias=mv[rsl][:, 3:4], scale=mv[rsl][:, 2:3]),
            reads=rk + [("mv2", rsl), ("mv3", rsl)], writes=rk)

        def mk_tail(rsl=rsl, i=i, rk=rk):
            ops = []
            for hh in range(2):
                hs = slice(1024 * hh, 1024 * hh + 1024)
                ops.append(lambda hs=hs, hh=hh: sc.op("dve", lambda e: e.tensor_tensor(
                    rbuf[rsl][:, hs], rbuf[rsl][:, hs], gain[:, hs], ALU.mult),
                    reads=rk + ["gain"], writes=[("rbh", rsl, hh)]))
                ops.append(lambda hs=hs, hh=hh: sc.op("dve", lambda e: e.tensor_tensor(
                    rbuf[rsl][:, hs], rbuf[rsl][:, hs], lbias[:, hs], ALU.add),
                    reads=[("rbh", rsl, hh), "lbias"], writes=[("rbh", rsl, hh)]))
            ops.append(lambda: sc.dma("sp", lambda e: e.dma_start(out=out[128 * i:128 * i + 128, :], in_=rbuf[rsl][:]),
                                      reads=[("rbh", rsl, 0), ("rbh", rsl, 1)], writes=rk + [("out", i)], sem="ost%d" % rsl))
            return ops
        pend_tail = mk_tail()
    for op_ in pend_tail:
        op_()
    sc.op("sp", None, reads=[("out", i) for i in range(16)] + (["yscr"] if debug_y else []))
    sc.emit(nc)
    return nc


_CACHE = {}


def prep_inputs(x, w_in, w_out, t5_bias, na_rpb, ln_gain, ln_bias):
    x = np.asarray(x, np.float32)
    shared = dict(
        w_in=host_w_in(np.asarray(w_in, np.float32)[0]),
        w_out=host_w_out(np.asarray(w_out, np.float32)[0]),
        biasA=host_biasA(np.asarray(t5_bias, np.float32)),
        biasB=host_biasB(np.asarray(na_rpb, np.float32)[0]),
        lng=np.ascontiguousarray(np.broadcast_to(np.asarray(ln_gain, np.float32)[0][None, :], (128, D))),
        lnb=np.ascontiguousarray(np.broadcast_to(np.asarray(ln_bias, np.float32)[0][None, :], (128, D))),
        consts=host_consts(),
    )
    in_maps = []
    for b in range(8):
        m = dict(shared)
        m["x"] = np.ascontiguousarray(x[b])
        in_maps.append(m)
    return in_maps


def kernel(x, w_in, w_out, t5_bias, na_rpb, ln_gain, ln_bias):
    in_maps = prep_inputs(x, w_in, w_out, t5_bias, na_rpb, ln_gain, ln_bias)
    if "nc" not in _CACHE:
        _CACHE["nc"] = build()
    nc = _CACHE["nc"]
    res = run_bass_kernel_spmd(nc, in_maps, core_ids=list(range(8)))
    outs = [np.asarray(r["out"], np.float32) for r in res.results]
    return np.stack(outs, axis=0)
```

```python
import numpy as np
import concourse.bass as bass
import concourse.mybir as mybir
from concourse.bass_utils import run_bass_kernel_spmd

F32 = mybir.dt.float32
BF16 = mybir.dt.bfloat16
AF = mybir.ActivationFunctionType
ALU = mybir.AluOpType

S = 2048
D = 2048
NEG = -30000.0
SCALE = 0.125
ALPHA = 2.0 ** 0.25
LN_EPS = 1e-5
NTA = 7
NTB = 9

ENGS = ("pe", "act", "dve", "pool", "sp")


class Sched:
    def __init__(self):
        self.prog = {e: [] for e in ENGS}
        self.lw = {}
        self.lr = {}
        self.dma_cnt = {}

    def _deps(self, reads, writes):
        deps = []
        for k in reads:
            if k in self.lw:
                deps.append(self.lw[k])
        for k in writes:
            if k in self.lw:
                deps.append(self.lw[k])
            for p in self.lr.get(k, {}).values():
                deps.append(p)
        return deps

    def op(self, eng, fn, reads=(), writes=()):
        deps = self._deps(reads, writes)
        idx = len(self.prog[eng])
        self.prog[eng].append(dict(fn=fn, deps=deps, kind="op"))
        me = ("eng", eng, idx)
        for k in reads:
            self.lr.setdefault(k, {})[eng] = me
        for k in writes:
            self.lw[k] = me
            self.lr[k] = {}
        return me

    def dma(self, queue, fn, reads=(), writes=(), sem=None):
        deps = self._deps(reads, writes)
        cnt = self.dma_cnt.get(sem, 0) + 1
        self.dma_cnt[sem] = cnt
        self.prog[queue].append(dict(fn=fn, deps=deps, kind="dma", sem=sem))
        me = ("dma", sem, cnt)
        for k in reads:
            self.lr.setdefault(k, {})["dma:" + sem] = me
        for k in writes:
            self.lw[k] = me
            self.lr[k] = {}
        return me

    def barrier(self):
        deps = []
        for e in ENGS:
            for idx in range(len(self.prog[e]) - 1, -1, -1):
                r = self.prog[e][idx]
                if r["kind"] == "op" and r["fn"] is not None:
                    deps.append(("eng", e, idx))
                    break
        for s, c in self.dma_cnt.items():
            deps.append(("dma", s, c))
        for e in ENGS:
            self.prog[e].append(dict(fn=None, deps=list(deps), kind="wait"))

    def emit(self, nc):
        need = {e: set() for e in ENGS}
        for e in ENGS:
            for rec in self.prog[e]:
                for d in rec["deps"]:
                    if d[0] == "eng":
                        assert self.prog[d[1]][d[2]]["kind"] == "op", (d, self.prog[d[1]][d[2]])
                        need[d[1]].add(d[2])
        val = {e: {idx: r + 1 for r, idx in enumerate(sorted(need[e]))} for e in ENGS}
        sems = {e: nc.alloc_semaphore("s_" + e) for e in ENGS}
        dsems = {s: nc.alloc_semaphore("d_%d" % i) for i, s in enumerate(self.dma_cnt)}
        prog = self.prog

        def run(e, eng):
            seen = {}
            for idx, rec in enumerate(prog[e]):
                for d in rec["deps"]:
                    if d[0] == "eng":
                        if d[1] == e and e == "pe":
                            continue
                        key = ("e", d[1])
                        s = sems[d[1]]
                        v = val[d[1]][d[2]]
                    else:
                        key = ("d", d[1])
                        s = dsems[d[1]]
                        v = 16 * d[2]
                    if seen.get(key, 0) >= v:
                        continue
                    eng.wait_ge(s, v)
                    seen[key] = v
                if rec["fn"] is None:
                    continue
                ins = rec["fn"](eng)
                if rec["kind"] == "dma":
                    ins.then_inc(dsems[rec["sem"]], 16)
                elif idx in val[e]:
                    ins.then_inc(sems[e], 1)

        with nc.Block() as blk:
            @blk.tensor
            def _(eng):
                run("pe", eng)

            @blk.scalar
            def _(eng):
                run("act", eng)

            @blk.vector
            def _(eng):
                run("dve", eng)

            @blk.gpsimd
            def _(eng):
                run("pool", eng)

            @blk.sync
            def _(eng):
                run("sp", eng)


def blocks_A():
    groups = []
    for t in range(16):
        blks = []
        if t == 0:
            blks.append(dict(k=(0, 1, 64), bt=5, vt=0))
        else:
            blks.append(dict(k=(128 * t - 64, 1, 128), bt=0, vt=t))
        if t == 15:
            blks.append(dict(k=(128 * t + 64, 1, 64), bt=1, vt=16))
        else:
            blks.append(dict(k=(128 * t + 64, 1, 128), bt=1, vt=t + 1))
        groups.append(dict(branch=1, obank=t // 4, ocol=128 * (t % 4), q=(128 * t, 1), blocks=blks))
    for c in range(4):
        for t in range(4):
            blks = []
            if t == 0:
                blks.append(dict(k=(c, 4, 64), bt=6, vt=17 + 5 * c))
            else:
                blks.append(dict(k=(c + 4 * (128 * t - 64), 4, 128), bt=2, vt=17 + 5 * c + t))
            if t == 3:
                blks.append(dict(k=(c + 4 * (128 * t + 64), 4, 64), bt=3, vt=17 + 5 * c + 4))
            else:
                blks.append(dict(k=(c + 4 * (128 * t + 64), 4, 128), bt=3, vt=17 + 5 * c + t + 1))
            groups.append(dict(branch=2, obank=c, ocol=128 * t, q=(c + 4 * 128 * t, 4), blocks=blks))
    for r in range(16):
        groups.append(dict(branch=3, obank=r // 4, ocol=128 * (r % 4), q=(r, 16),
                           blocks=[dict(k=(r, 16, 128), bt=4, vt=37 + r)]))
    return groups


def vtiles_A():
    tiles = []
    for j in range(17):
        if j == 0:
            tiles.append((0, (0, 1, 64)))
        elif j == 16:
            tiles.append((16, (1984, 1, 64)))
        else:
            tiles.append((j, (128 * j - 64, 1, 128)))
    for c in range(4):
        for j in range(5):
            if j == 0:
                tiles.append((17 + 5 * c, (c, 4, 64)))
            elif j == 4:
                tiles.append((17 + 5 * c + 4, (c + 4 * 448, 4, 64)))
            else:
                tiles.append((17 + 5 * c + j, (c + 4 * (128 * j - 64), 4, 128)))
    for r in range(16):
        tiles.append((37 + r, (r, 16, 128)))
    return tiles


def _rs(rho):
    return min(max(rho - 4, 0), 24)


def blocks_B():
    groups = []
    for i in range(16):
        blks = []
        for kt in range(16):
            pat = []
            for a in (0, 1):
                for b in (0, 1):
                    kr, qr = 2 * kt + a, 2 * i + b
                    pat.append(_rs(qr) <= kr <= _rs(qr) + 7)
            if not any(pat):
                continue
            dl = kt - i
            interior = [(-4 <= 2 * dl + a - b <= 3) for a in (0, 1) for b in (0, 1)]
            if pat == interior and -2 <= dl <= 2:
                bt = dl + 2
            elif all(pat) and dl == 2:
                bt = 5
            elif all(pat) and dl == 3:
                bt = 6
            elif all(pat) and dl == -3:
                bt = 7
            elif all(pat) and dl == -2:
                bt = 8
            else:
                raise AssertionError((i, kt, pat))
            blks.append(dict(k=(128 * kt, 1, 128), bt=bt, vt=kt))
        groups.append(dict(branch=0, obank=i // 4, ocol=128 * (i % 4), q=(128 * i, 1), blocks=blks))
    return groups


def vtiles_B():
    return [(j, (128 * j, 1, 128)) for j in range(16)]


def _t5_buckets(rel):
    nb, maxd = 32, 1024
    half = nb // 2
    max_exact = half // 2
    n = np.abs(rel)
    large = max_exact + (np.log(np.maximum(n, 1) / max_exact)
                         / np.log(maxd / max_exact) * (half - max_exact)).astype(np.int64)
    large = np.minimum(large, half - 1)
    return ((rel > 0).astype(np.int64) * half + np.where(n < max_exact, n, large)).astype(np.int64)


def host_biasA(t5):
    k = np.arange(128)[:, None]
    q = np.arange(128)[None, :]
    specs = []
    for d in (1, 4):
        specs.append((k - 64 - q, (k >= q), d))
        specs.append((k + 64 - q, (k <= q), d))
    specs.append((k - q, (np.abs(k - q) <= 64), 16))
    for d in (1, 4):
        specs.append((k - q, ((k - q) >= -64) & (k < 64), d))
    out = np.full((16, 128, NTA, 128), NEG, np.float32)
    for ti, (rel, valid, d) in enumerate(specs):
        bk = _t5_buckets(rel * d)
        vals = t5[bk]
        for h in range(16):
            out[h, :, ti, :] = np.where(valid, vals[:, :, h], np.float32(NEG))
    out = out.reshape(8, 2, 128, NTA, 128).transpose(0, 2, 1, 3, 4).reshape(8, 128, 2 * NTA * 128)
    return np.ascontiguousarray(out)


def host_biasB(rpb):
    a = (np.arange(128) // 64)[:, None]
    cp = (np.arange(128) % 64)[:, None]
    b = (np.arange(128) // 64)[None, :]
    c = (np.arange(128) % 64)[None, :]
    qs = np.clip(c - 8, 0, 48)
    colv = (cp >= qs) & (cp < qs + 16)
    dcol = np.clip(cp - c, -15, 15) + 15
    variants = [(-2, "int"), (-1, "int"), (0, "int"), (1, "int"), (2, "int"),
                (2, "all"), (3, "all"), (-3, "all"), (-2, "all")]
    out = np.full((16, 128, NTB, 128), NEG, np.float32)
    for ti, (dl, kind) in enumerate(variants):
        dr = 2 * dl + a - b
        rowv = ((dr >= -4) & (dr <= 3)) if kind == "int" else np.ones_like(dr, bool)
        ri = np.clip(dr + 7, 0, 14)
        valid = rowv & colv
        for h in range(16):
            out[h, :, ti, :] = np.where(valid, rpb[h][ri, dcol], np.float32(NEG))
    out = out.reshape(8, 2, 128, NTB, 128).transpose(0, 2, 1, 3, 4).reshape(8, 128, 2 * NTB * 128)
    return np.ascontiguousarray(out)


def host_w_in(w_in):
    w = w_in.reshape(16, 128, 2, 4, 8, 128)
    w = w.transpose(2, 4, 1, 0, 3, 5)
    return np.ascontiguousarray(w.reshape(16, 128, 16 * 512))


def host_w_out(w_out):
    w = w_out.reshape(16, 128, 2048).transpose(1, 0, 2)
    return np.ascontiguousarray(w.reshape(128, 16 * 2048))


def host_consts():
    c = np.zeros((128, 128 + 192 + 64), np.float32)
    c[:, 320:384] = 1.0
    c[np.arange(128), np.arange(128)] = 1.0
    c[np.arange(64), 128 + 64 + np.arange(64)] = 1.0
    return c


def build(pairs=tuple(range(16)), debug_y=False):
    npairs = len(pairs)
    nc = bass.Bass("TRN2", target_bir_lowering=False)
    x = nc.dram_tensor("x", [S, D], F32, kind="ExternalInput").ap()
    w_in = nc.dram_tensor("w_in", [16, 128, 8192], F32, kind="ExternalInput").ap()
    w_out = nc.dram_tensor("w_out", [128, 16 * 2048], F32, kind="ExternalInput").ap()
    biasA = nc.dram_tensor("biasA", [8, 128, 2 * NTA * 128], F32, kind="ExternalInput").ap()
    biasB = nc.dram_tensor("biasB", [8, 128, 2 * NTB * 128], F32, kind="ExternalInput").ap()
    lng = nc.dram_tensor("lng", [128, D], F32, kind="ExternalInput").ap()
    lnb = nc.dram_tensor("lnb", [128, D], F32, kind="ExternalInput").ap()
    consts = nc.dram_tensor("consts", [128, 384], F32, kind="ExternalInput").ap()
    out = nc.dram_tensor("out", [S, D], F32, kind="ExternalOutput").ap()
    yscr_t = nc.dram_tensor("yscr", [16, 128, 16, 128], BF16,
                            kind=("ExternalOutput" if debug_y else "Internal"))
    yscr = yscr_t.ap()

    sb = nc.alloc_sbuf_tensor

    def alias(name, shape, dtype, base, off=0):
        return nc.alloc_sbuf_tensor_at(name, shape, dtype, offset=nc.lookup_mloc(base).addr + off)

    BIG = sb("BIG", [128, 16, 2048], BF16)
    W = [sb("W%d" % i, [128, 8192], BF16) for i in range(2)]
    ACC = [sb("ACC%d" % i, [128, 2048], F32) for i in range(2)]
    G = [sb("G%d" % i, [128, 2048], BF16) for i in range(2)]
    QT = [sb("QT%d" % i, [128, 2048], BF16) for i in range(2)]
    KT = [sb("KT%d" % i, [128, 2048], BF16) for i in range(2)]
    VT = [sb("VT%d" % i, [128, 2048], BF16) for i in range(2)]
    Vt = [sb("Vt%d" % i, [128, 53, 2, 65], BF16) for i in range(2)]
    biasT = [sb("biasT%d" % i, [128, 2 * NTB * 128], F32) for i in range(2)]
    Sb = [sb("Sb%d" % i, [128, 512], F32) for i in range(2)]
    PT = [sb("PT%d" % i, [128, 512], BF16) for i in range(5)]
    yn = [sb("yn%d" % i, [128, 512], BF16) for i in range(2)]
    ybuf = [sb("ybuf%d" % i, [128, 512], BF16) for i in range(2)]
    cst = sb("cst", [128, 384], BF16)
    hl = sb("hl", [128, 2, 2, 512], BF16)
    ones32 = sb("ones32", [128, 64], F32)
    stt = [sb("stt%d" % i, [128, 24], F32) for i in range(2)]
    mv = [sb("mv%d" % i, [128, 4], F32) for i in range(2)]
    xin = [alias("xin0", [128, 2048], BF16, ACC[0]), alias("xin1", [128, 2048], BF16, ACC[0], 4096)]
    xres = [alias("xres0", [128, 2048], F32, QT[0]), alias("xres1", [128, 2048], F32, G[0])]
    rbuf = [alias("rbuf0", [128, 2048], F32, KT[0]), alias("rbuf1", [128, 2048], F32, VT[0])]
    yTt = [alias("yTt0", [128, 16, 128], BF16, Vt[0]), alias("yTt1", [128, 16, 128], BF16, Vt[0], 4096)]

    ad = lambda t: nc.lookup_mloc(t).addr
    assert ad(G[1]) == ad(G[0]) + 4096 and ad(QT[1]) == ad(QT[0]) + 4096 and ad(KT[1]) == ad(KT[0]) + 4096 and ad(VT[1]) == ad(VT[0]) + 4096
    P = [nc.alloc_psum_tensor("P%d" % i, [128, 512], F32) for i in range(2)]
    SP = [nc.alloc_psum_tensor("S%d" % i, [128, 512], F32) for i in range(4)]
    OP = [nc.alloc_psum_tensor("O%d" % i, [128, 512], F32) for i in range(2)]
    TP = [P[i][:, :].bitcast(BF16) for i in range(2)]

    ident = cst[:, 0:128]
    E_lo = cst[0:64, 192:320]
    E_sh = cst[0:64, 128:256]

    def AP(t, off, dims):
        return bass.AP(t, off, [list(d) for d in dims])

    def pstride(t):
        return t[:].ap[0][0]

    def col_ap(t, prow, nrow, start, step, n):
        return AP(t, prow * pstride(t) + start, [[pstride(t), nrow], [step, n]])

    sc = Sched()
    cnt = dict(p=0, s=0, o=0, t=0, sb=0, pt=0, yb=0)

    def nxt(k, n):
        v = cnt[k] % n
        cnt[k] += 1
        return v

    sc.dma("pool", lambda e: e.dma_start(out=cst[:], in_=consts), writes=["cst"], sem="cst")
    sc.op("dve", lambda e: e.memset(ones32[:], 1.0), writes=["ones32"])
    for b in range(2):
        sc.op("dve", lambda e, b=b: e.memset(Vt[b][:, :, :, 64:65], 1.0), writes=[("Vt", b)])

    def load_w(p, sl):
        for c4 in range(4):
            sc.dma("pool", lambda e, sl=sl, p=p, c4=c4: e.dma_start(
                out=W[sl][:, 2048 * c4:2048 * c4 + 2048], in_=w_in[p, :, 2048 * c4:2048 * c4 + 2048]),
                writes=[("W", sl)], sem="W%d" % sl)

    grpsA = blocks_A()
    grpsB = blocks_B()
    vtA = vtiles_A()
    vtB = vtiles_B()

    def proj_group(pi, oi, ck):
        p = pairs[pi]
        wsl = pi % 2
        bsl = pi % 2
        pb = nxt("p", 2)
        for dc in range(16):
            sc.op("pe", lambda e, pb=pb, wsl=wsl, dc=dc, oi=oi, ck=ck: e.matmul(
                P[pb][:, :], W[wsl][:, 512 * dc + 128 * oi:512 * dc + 128 * oi + 128],
                BIG[:, dc, 512 * ck:512 * ck + 512], start=(dc == 0), stop=(dc == 15)),
                reads=[("W", wsl), ("xT", 0, ck), ("xT", 1, ck)], writes=[("P", pb)])
        cs = slice(512 * ck, 512 * ck + 512)
        if oi == 0:
            sc.op("act", lambda e, pb=pb, cs=cs: e.copy(QT[bsl][:, cs], P[pb][:, :]),
                  reads=[("P", pb)], writes=[("QT", bsl, ck)])
        elif oi == 1:
            sc.op("act", lambda e, pb=pb, cs=cs: e.copy(KT[bsl][:, cs], P[pb][:, :]),
                  reads=[("P", pb)], writes=[("KT", bsl, ck)])
        elif oi == 2:
            sc.op("dve", lambda e, pb=pb, cs=cs: e.tensor_copy(VT[bsl][:, cs], P[pb][:, :]),
                  reads=[("P", pb)], writes=[("VT", bsl, ck)])
        else:
            sc.op("act", lambda e, pb=pb, cs=cs: e.activation(G[bsl][:, cs], P[pb][:, :], AF.Silu),
                  reads=[("P", pb)], writes=[("G", bsl, ck)])

    def units_P(pi):
        p = pairs[pi]
        isA = p < 8
        bsl = pi % 2
        units = []

        def u_first():
            if isA:
                sc.dma("sp", lambda e: e.dma_start(out=biasT[bsl][:, 0:2 * NTA * 128], in_=biasA[p]),
                       writes=[("bias", bsl)], sem="bias%d" % bsl)
            else:
                sc.dma("sp", lambda e: e.dma_start(out=biasT[bsl][:, 0:2 * NTB * 128], in_=biasB[p - 8]),
                       writes=[("bias", bsl)], sem="bias%d" % bsl)
        units.append(u_first)
        if pi == 0:
            pass
        else:
            for oi in (2, 1, 0, 3):
                for ck in range(4):
                    units.append(lambda oi=oi, ck=ck: proj_group(pi, oi, ck))
        vts = vtA if isA else vtB
        full = [v for v in vts if v[1][2] == 128]
        edge = [v for v in vts if v[1][2] == 64]
        VTk = [("VT", bsl, c) for c in range(4)]
        i0 = 0
        runs = []
        while i0 < len(full):
            run = [full[i0]]
            while len(run) < 8 and i0 + len(run) < len(full) and full[i0 + len(run)][0] == run[-1][0] + 1:
                run.append(full[i0 + len(run)])
            i0 += len(run)
            runs.append(run)

        def u_run(run):
            tb = nxt("p", 2)
            for j, (ti, (st, sp_, nk)) in enumerate(run):
                sc.op("pe", lambda e, tb=tb, j=j, st=st, sp_=sp_: e.transpose(
                    TP[tb][:, 128 * j:128 * j + 128], col_ap(VT[bsl], 0, 128, st, sp_, 128), ident),
                    reads=VTk + ["cst"], writes=[("P", tb)])
            n = len(run)
            t0 = run[0][0]
            dst = Vt[bsl][:, t0:t0 + n, :, 0:64]
            src = TP[tb][:, 0:128 * n].rearrange("p (a h c) -> p a h c", a=n, h=2)
            sc.op("act", lambda e, dst=dst, src=src: e.copy(dst, src),
                  reads=[("P", tb)], writes=[("Vt", bsl)])

        def u_edges(es):
            tb = nxt("p", 2)
            for j, (ti, (st, sp_, nk)) in enumerate(es):
                sc.op("pe", lambda e, tb=tb, j=j, st=st, sp_=sp_: e.transpose(
                    TP[tb][0:64, 128 * j:128 * j + 128], col_ap(VT[bsl], 0, 128, st, sp_, 64), ident),
                    reads=VTk + ["cst"], writes=[("P", tb)])
            for j, (ti, _) in enumerate(es):
                dst = Vt[bsl][0:64, ti, :, 0:64]
                src = TP[tb][0:64, 128 * j:128 * j + 128].rearrange("p (h c) -> p h c", h=2)
                sc.op("act", lambda e, dst=dst, src=src: e.copy(dst, src),
                      reads=[("P", tb)], writes=[("Vt", bsl)])
        for run in runs:
            units.append(lambda run=run: u_run(run))
        for i0 in range(0, len(edge), 8):
            units.append(lambda es=edge[i0:i0 + 8]: u_edges(es))
        if pi + 1 < npairs:
            units.insert(1, lambda: load_w(pairs[pi + 1], (pi + 1) % 2))
        return units

    def units_G(pi):
        return [lambda ck=ck: proj_group(pi, 3, ck) for ck in range(4)]

    def units_T(pi):
        p = pairs[pi]
        isA = p < 8
        bsl = pi % 2
        NT = NTA if isA else NTB
        grps = grpsA if isA else grpsB
        QTk = [("QT", bsl, c) for c in range(4)]
        KTk = [("KT", bsl, c) for c in range(4)]
        units = []
        state = {}
        allA, allB = [], []
        for e_ in range(2):
            prow = 64 * e_
            acc = ACC[e_]
            ps_acc = pstride(acc)
            order = sorted(range(len(grps)), key=lambda gi: (grps[gi]["branch"], grps[gi]["obank"], grps[gi]["ocol"]))
            stream = []
            for gi in order:
                g = grps[gi]
                nb = len(g["blocks"])
                for bi, b in enumerate(g["blocks"]):
                    stream.append(dict(g=g, b=b, first=(bi == 0), last=(bi == nb - 1)))
            packs = []
            cur = []
            for it in stream:
                if it["b"]["k"][2] == 64:
                    if cur:
                        packs.append(cur)
                        cur = []
                    packs.append([it])
                else:
                    cur.append(it)
                    if len(cur) == 4:
                        packs.append(cur)
                        cur = []
            if cur:
                packs.append(cur)

            def flush_obank(key, ob, e_=e_, acc=acc, ps_acc=ps_acc):
                br, obk = key
                src = OP[ob][0:65, :]
                if br in (0, 1):
                    dst = acc[0:65, 512 * obk:512 * obk + 512]
                    sc.op("act", lambda e, dst=dst, src=src: e.copy(dst, src),
                          reads=[("O", ob)], writes=[("ACC", e_)])
                else:
                    if br == 2:
                        view = AP(acc, obk, [[ps_acc, 65], [512, 4], [4, 128]])
                    else:
                        view = AP(acc, 4 * obk, [[ps_acc, 65], [1, 4], [16, 128]])
                    src3 = src.rearrange("p (a b) -> p a b", a=4)
                    sc.op("dve", lambda e, view=view, src3=src3: e.tensor_tensor(view, src3, view, ALU.add),
                          reads=[("O", ob), ("ACC", e_)], writes=[("ACC", e_)])

            def u_packA(pk, e_=e_, prow=prow):
                nk = pk[0]["b"]["k"][2]
                nblk = len(pk)
                sbk = nxt("s", 4)
                for j, it in enumerate(pk):
                    ks, kp, _ = it["b"]["k"]
                    qs, qp = it["g"]["q"]
                    sc.op("pe", lambda e, sbk=sbk, j=j, ks=ks, kp=kp, qs=qs, qp=qp: e.matmul(
                        SP[sbk][0:nk, 128 * j:128 * j + 128],
                        col_ap(KT[bsl], prow, 64, ks, kp, nk), col_ap(QT[bsl], prow, 64, qs, qp, 128),
                        start=True, stop=True),
                        reads=QTk + KTk, writes=[("S", sbk)])
                sbs = nxt("sb", 2)
                bts = [it["b"]["bt"] for it in pk]
                j0 = 0
                while j0 < nblk:
                    j1 = j0 + 1
                    while j1 < nblk and bts[j1] == bts[j1 - 1] + 1:
                        j1 += 1
                    boff = (e_ * NT + bts[j0]) * 128
                    n = 128 * (j1 - j0)
                    sc.op("dve", lambda e, sbs=sbs, sbk=sbk, j0=j0, n=n, boff=boff: e.scalar_tensor_tensor(
                        Sb[sbs][0:nk, 128 * j0:128 * j0 + n], SP[sbk][0:nk, 128 * j0:128 * j0 + n], SCALE,
                        biasT[bsl][0:nk, boff:boff + n], ALU.mult, ALU.add),
                        reads=[("S", sbk), ("bias", bsl)], writes=[("Sb", sbs)])
                    j0 = j1
                pts = nxt("pt", 5)
                sc.op("act", lambda e, pts=pts, sbs=sbs: e.activation(
                    PT[pts][0:nk, 0:128 * nblk], Sb[sbs][0:nk, 0:128 * nblk], AF.Exp),
                    reads=[("Sb", sbs)], writes=[("PT", pts)])
                pk[0]["pts"] = pts

            def u_packB(pk, e_=e_, flush_obank=flush_obank):
                nk = pk[0]["b"]["k"][2]
                pts = pk[0]["pts"]
                for j, it in enumerate(pk):
                    g = it["g"]
                    key = (e_, g["branch"], g["obank"])
                    if key != state.get("cur"):
                        if state.get("cur") is not None:
                            state["flush"](state["cur"][1:], state["ob"])
                        state["cur"] = key
                        state["ob"] = nxt("o", 2)
                        state["flush"] = flush_obank
                    ob = state["ob"]
                    vt = it["b"]["vt"]
                    oc = g["ocol"]
                    sc.op("pe", lambda e, ob=ob, oc=oc, vt=vt, pts=pts, j=j, f=it["first"], l=it["last"]: e.matmul(
                        OP[ob][0:65, oc:oc + 128], Vt[bsl][0:nk, vt, e_, :], PT[pts][0:nk, 128 * j:128 * j + 128],
                        start=f, stop=l),
                        reads=[("Vt", bsl), ("PT", pts)], writes=[("O", ob)])

            for pk in packs:
                pk = [dict(it) for it in pk]
                allA.append(lambda pk=pk, f=u_packA: f(pk))
                allB.append(lambda pk=pk, f=u_packB: f(pk))

        LAG = 4
        for k in range(len(allA)):
            units.append(allA[k])
            if k >= LAG:
                units.append(allB[k - LAG])
        for k in range(max(0, len(allA) - LAG), len(allA)):
            units.append(allB[k])

        def u_flush_last():
            state["flush"](state["cur"][1:], state["ob"])
            state["cur"] = None
        units.append(u_flush_last)

        def fS1(ck):
            cs = slice(512 * ck, 512 * ck + 512)
            for e_ in range(2):
                sc.op("act", lambda e, e_=e_: e.activation(ACC[e_][64:65, cs], ACC[e_][64:65, cs], AF.Ln),
                      reads=[("ACC", e_)], writes=[("ACCd", e_, ck)])
                sc.op("act", lambda e, e_=e_: e.activation(ACC[e_][64:65, cs], ACC[e_][64:65, cs], AF.Exp, scale=-1.0),
                      reads=[("ACCd", e_, ck)], writes=[("ACCd", e_, ck)])
                sc.op("act", lambda e, e_=e_: e.copy(hl[64:65, 0, e_, :], ACC[e_][64:65, cs]),
                      reads=[("ACCd", e_, ck)], writes=[("hl", 0, e_)])
            for e_ in range(2):
                sc.op("pool", lambda e, e_=e_: e.tensor_tensor(
                    hl[64:65, 1, e_, :], ACC[e_][64:65, cs], hl[64:65, 0, e_, :], ALU.subtract),
                    reads=[("ACCd", e_, ck), ("hl", 0, e_)], writes=[("hl", 1, e_)])

        def fS2(ck):
            cs = slice(512 * ck, 512 * ck + 512)
            pbs = []
            for e_ in range(2):
                pb = nxt("p", 2)
                pbs.append(pb)
                sc.op("pe", lambda e, pb=pb, e_=e_: e.matmul(
                    P[pb][0:64, :], cst[64:65, 320:384], hl[64:65, 0, e_, :], start=True, stop=False),
                    reads=["cst", ("hl", 0, e_)], writes=[("P", pb)])
                sc.op("pe", lambda e, pb=pb, e_=e_: e.matmul(
                    P[pb][0:64, :], cst[64:65, 320:384], hl[64:65, 1, e_, :], start=False, stop=True),
                    reads=["cst", ("hl", 1, e_)], writes=[("P", pb)])
            for e_ in range(2):
                pb = pbs[e_]
                sc.op("dve", lambda e, pb=pb, e_=e_: e.tensor_tensor(
                    yn[e_][0:64, :], ACC[e_][0:64, cs], P[pb][0:64, :], ALU.mult),
                    reads=[("P", pb), ("ACC", e_)], writes=[("yn", e_)])

        def fS3(ck):
            cs = slice(512 * ck, 512 * ck + 512)
            pb = nxt("p", 2)
            sc.op("pe", lambda e, pb=pb: e.matmul(P[pb][:, :], E_lo, yn[0][0:64, :], start=True, stop=False),
                  reads=["cst", ("yn", 0)], writes=[("P", pb)])
            sc.op("pe", lambda e, pb=pb: e.matmul(P[pb][:, :], E_sh, yn[1][0:64, :], start=False, stop=True),
                  reads=["cst", ("yn", 1)], writes=[("P", pb)])
            ys = nxt("yb", 2)
            sc.op("dve", lambda e, pb=pb, ys=ys: e.tensor_tensor(
                ybuf[ys][:, :], P[pb][:, :], G[bsl][:, cs], ALU.mult),
                reads=[("P", pb), ("G", bsl, ck)], writes=[("ybuf", ys)])
            dst = yscr[4 * ck:4 * ck + 4, :, p, :].rearrange("i e t -> e i t")
            src = ybuf[ys][:, :].rearrange("e (i t) -> e i t", i=4)
            sc.dma("sp", lambda e, dst=dst, src=src: e.dma_start(out=dst, in_=src),
                   reads=[("ybuf", ys)], writes=["yscr"], sem="yst%d" % ys)

        seq = [("1", 0), ("2", 0), ("1", 1), ("3", 0), ("2", 1), ("1", 2), ("3", 1), ("2", 2), ("1", 3),
               ("3", 2), ("2", 3), ("3", 3)]
        fmap = {"1": fS1, "2": fS2, "3": fS3}
        for kind, ck in seq:
            units.append(lambda kind=kind, ck=ck: fmap[kind](ck))
        return units

    def phase0():
        for i in range(16):
            sl = i % 2
            sc.dma("pool", lambda e, sl=sl, i=i: e.dma_start(out=xin[sl][:], in_=x[128 * i:128 * i + 128, :]),
                   writes=[("xin", sl)], sem="xin%d" % sl)
            if i == 1 and npairs > 0:
                load_w(pairs[0], 0)
            for g in range(2):
                tb = nxt("p", 2)
                for j in range(8):
                    dc = 8 * g + j
                    sc.op("pe", lambda e, tb=tb, j=j, sl=sl, dc=dc: e.transpose(
                        TP[tb][:, 128 * j:128 * j + 128], xin[sl][:, 128 * dc:128 * dc + 128], ident),
                        reads=[("xin", sl), "cst"], writes=[("P", tb)])
                dst = BIG[:, 8 * g:8 * g + 8, 128 * i:128 * i + 128]
                src = TP[tb][:, :].rearrange("p (a b) -> p a b", a=8)
                if g == 0:
                    sc.op("act", lambda e, dst=dst, src=src: e.copy(dst, src),
                          reads=[("P", tb)], writes=[("xT", g, i // 4)])
                else:
                    sc.op("dve", lambda e, dst=dst, src=src: e.tensor_copy(dst, src),
                          reads=[("P", tb)], writes=[("xT", g, i // 4)])
            if i % 4 == 3 and npairs > 0:
                for oi in (2, 1, 0, 3):
                    proj_group(0, oi, i // 4)


    phase0()
    if npairs > 0:
        for u in units_P(0):
            u()
    for pi in range(npairs):
        tu = units_T(pi)
        pu = units_P(pi + 1) if pi + 1 < npairs else []
        nT, nP = len(tu), len(pu)
        span = max(1, int(nT * 0.72))
        ip = 0
        for it_, u in enumerate(tu):
            u()
            while ip < nP and (ip + 1) * span <= (it_ + 1) * nP:
                pu[ip]()
                ip += 1
        while ip < nP:
            pu[ip]()
            ip += 1
        if pi + 1 == npairs - 1 or npairs == 1:
            for c8 in range(8):
                sc.dma("pool", lambda e, c8=c8: e.dma_start(
                    out=BIG[:, 2 * c8:2 * c8 + 2, :],
                    in_=w_out[:, 4096 * c8:4096 * c8 + 4096].rearrange("p (a b) -> p a b", a=2)),
                    writes=[("WO", c8)] + [("xT", g_, c_) for g_ in range(2) for c_ in range(4)], sem="WO%d" % c8)

    sc.barrier()
    WO = BIG
    WOk = [("WO", c8) for c8 in range(8)]
    gain = ACC[0]
    lbias = ACC[1]
    sc.dma("sp", lambda e: e.dma_start(out=gain[:], in_=lng), writes=["gain"], sem="gain")
    sc.dma("sp", lambda e: e.dma_start(out=lbias[:], in_=lnb), writes=["lbias"], sem="lbias")
    def p2_loads(i):
        ysl = i % 2
        xsl = i % 2
        sc.dma("sp", lambda e, ysl=ysl, i=i: e.dma_start(out=yTt[ysl][:], in_=yscr[i]),
               reads=["yscr"], writes=[("yTt", ysl)], sem="yTt%d" % ysl)
        sc.dma("sp", lambda e, i=i, xsl=xsl: e.dma_start(out=xres[xsl][:], in_=x[128 * i:128 * i + 128, :]),
               writes=[("xres", xsl)], sem="xres%d" % xsl)

    p2_loads(0)
    pend_tail = []
    for i in range(16):
        ysl = i % 2
        rsl = i % 2
        xsl = i % 2
        if i + 1 < 16:
            p2_loads(i + 1)
        for n in range(4):
            cs = slice(512 * n, 512 * n + 512)
            pb = nxt("p", 2)
            for ec in range(16):
                sc.op("pe", lambda e, pb=pb, ysl=ysl, ec=ec, cs=cs: e.matmul(
                    P[pb][:, :], yTt[ysl][:, ec, :], WO[:, ec, cs], start=(ec == 0), stop=(ec == 15)),
                    reads=[("yTt", ysl), ("WO", ec // 2)], writes=[("P", pb)])
            sc.op("dve", lambda e, pb=pb, rsl=rsl, cs=cs, xsl=xsl: e.scalar_tensor_tensor(
                rbuf[rsl][:, cs], xres[xsl][:, cs], ALPHA, P[pb][:, :], ALU.mult, ALU.add),
                reads=[("P", pb), ("xres", xsl)], writes=[("rbuf", rsl, n)])
            sc.op("dve", lambda e, rsl=rsl, n=n, cs=cs: e.bn_stats(stt[rsl][:, 6 * n:6 * n + 6], rbuf[rsl][:, cs]),
                  reads=[("rbuf", rsl, n)], writes=[("stt", rsl, n)])
            if pend_tail:
                pend_tail.pop(0)()
                if n == 3:
                    while pend_tail:
                        pend_tail.pop(0)()
        rk = [("rbuf", rsl, n) for n in range(4)]
        sc.op("dve", lambda e, rsl=rsl: e.bn_aggr(mv[rsl][:, 0:2], stt[rsl][:, 0:24]),
              reads=[("stt", rsl, n) for n in range(4)], writes=[("mv", rsl)])
        sc.op("dve", lambda e, rsl=rsl: e.tensor_scalar(
            mv[rsl][:, 2:3], mv[rsl][:, 1:2], LN_EPS, None, ALU.add),
            reads=[("mv", rsl)], writes=[("mv2", rsl)])
        sc.op("act", lambda e, rsl=rsl: e.sqrt(mv[rsl][:, 2:3], mv[rsl][:, 2:3]),
              reads=[("mv2", rsl)], writes=[("mv2", rsl)])
        sc.op("dve", lambda e, rsl=rsl: e.reciprocal(mv[rsl][:, 2:3], mv[rsl][:, 2:3]),
              reads=[("mv2", rsl)], writes=[("mv2", rsl)])
        sc.op("dve", lambda e, rsl=rsl: e.scalar_tensor_tensor(
            mv[rsl][:, 3:4], mv[rsl][:, 0:1], -1.0, mv[rsl][:, 2:3], ALU.mult, ALU.mult),
            reads=[("mv", rsl), ("mv2", rsl)], writes=[("mv3", rsl)])
        sc.op("act", lambda e, rsl=rsl: e.activation(
            rbuf[rsl][:, :], rbuf[rsl][:, :], AF.Identity, bias=mv[rsl][:, 3:4], scale=mv[rsl][:, 2:3]),
            reads=rk + [("mv2", rsl), ("mv3", rsl)], writes=rk)

        def mk_tail(rsl=rsl, i=i, rk=rk):
            ops = []
            for hh in range(2):
                hs = slice(1024 * hh, 1024 * hh + 1024)
                ops.append(lambda hs=hs, hh=hh: sc.op("dve", lambda e: e.tensor_tensor(
                    rbuf[rsl][:, hs], rbuf[rsl][:, hs], gain[:, hs], ALU.mult),
                    reads=rk + ["gain"], writes=[("rbh", rsl, hh)]))
                ops.append(lambda hs=hs, hh=hh: sc.op("dve", lambda e: e.tensor_tensor(
                    rbuf[rsl][:, hs], rbuf[rsl][:, hs], lbias[:, hs], ALU.add),
                    reads=[("rbh", rsl, hh), "lbias"], writes=[("rbh", rsl, hh)]))
            ops.append(lambda: sc.dma("sp", lambda e: e.dma_start(out=out[128 * i:128 * i + 128, :], in_=rbuf[rsl][:]),
                                      reads=[("rbh", rsl, 0), ("rbh", rsl, 1)], writes=rk + [("out", i)], sem="ost%d" % rsl))
            return ops
        pend_tail = mk_tail()
    for op_ in pend_tail:
        op_()
    sc.op("sp", None, reads=[("out", i) for i in range(16)] + (["yscr"] if debug_y else []))
    sc.emit(nc)
    return nc


_CACHE = {}


def prep_inputs(x, w_in, w_out, t5_bias, na_rpb, ln_gain, ln_bias):
    x = np.asarray(x, np.float32)
    shared = dict(
        w_in=host_w_in(np.asarray(w_in, np.float32)[0]),
        w_out=host_w_out(np.asarray(w_out, np.float32)[0]),
        biasA=host_biasA(np.asarray(t5_bias, np.float32)),
        biasB=host_biasB(np.asarray(na_rpb, np.float32)[0]),
        lng=np.ascontiguousarray(np.broadcast_to(np.asarray(ln_gain, np.float32)[0][None, :], (128, D))),
        lnb=np.ascontiguousarray(np.broadcast_to(np.asarray(ln_bias, np.float32)[0][None, :], (128, D))),
        consts=host_consts(),
    )
    in_maps = []
    for b in range(8):
        m = dict(shared)
        m["x"] = np.ascontiguousarray(x[b])
        in_maps.append(m)
    return in_maps


def kernel(x, w_in, w_out, t5_bias, na_rpb, ln_gain, ln_bias):
    in_maps = prep_inputs(x, w_in, w_out, t5_bias, na_rpb, ln_gain, ln_bias)
    if "nc" not in _CACHE:
        _CACHE["nc"] = build()
    nc = _CACHE["nc"]
    res = run_bass_kernel_spmd(nc, in_maps, core_ids=list(range(8)))
    outs = [np.asarray(r["out"], np.float32) for r in res.results]
    return np.stack(outs, axis=0)
```

```python
import numpy as np
import concourse.bass as bass
import concourse.mybir as mybir
from concourse.bass_utils import run_bass_kernel_spmd

F32 = mybir.dt.float32
BF16 = mybir.dt.bfloat16
AF = mybir.ActivationFunctionType
ALU = mybir.AluOpType

S = 2048
D = 2048
NEG = -30000.0
SCALE = 0.125
ALPHA = 2.0 ** 0.25
LN_EPS = 1e-5
NTA = 7
NTB = 9

ENGS = ("pe", "act", "dve", "pool", "sp")


class Sched:
    def __init__(self):
        self.prog = {e: [] for e in ENGS}
        self.lw = {}
        self.lr = {}
        self.dma_cnt = {}

    def _deps(self, reads, writes):
        deps = []
        for k in reads:
            if k in self.lw:
                deps.append(self.lw[k])
        for k in writes:
            if k in self.lw:
                deps.append(self.lw[k])
            for p in self.lr.get(k, {}).values():
                deps.append(p)
        return deps

    def op(self, eng, fn, reads=(), writes=()):
        deps = self._deps(reads, writes)
        idx = len(self.prog[eng])
        self.prog[eng].append(dict(fn=fn, deps=deps, kind="op"))
        me = ("eng", eng, idx)
        for k in reads:
            self.lr.setdefault(k, {})[eng] = me
        for k in writes:
            self.lw[k] = me
            self.lr[k] = {}
        return me

    def dma(self, queue, fn, reads=(), writes=(), sem=None):
        deps = self._deps(reads, writes)
        cnt = self.dma_cnt.get(sem, 0) + 1
        self.dma_cnt[sem] = cnt
        self.prog[queue].append(dict(fn=fn, deps=deps, kind="dma", sem=sem))
        me = ("dma", sem, cnt)
        for k in reads:
            self.lr.setdefault(k, {})["dma:" + sem] = me
        for k in writes:
            self.lw[k] = me
            self.lr[k] = {}
        return me

    def barrier(self):
        deps = []
        for e in ENGS:
            for idx in range(len(self.prog[e]) - 1, -1, -1):
                r = self.prog[e][idx]
                if r["kind"] == "op" and r["fn"] is not None:
                    deps.append(("eng", e, idx))
                    break
        for s, c in self.dma_cnt.items():
            deps.append(("dma", s, c))
        for e in ENGS:
            self.prog[e].append(dict(fn=None, deps=list(deps), kind="wait"))

    def emit(self, nc):
        need = {e: set() for e in ENGS}
        for e in ENGS:
            for rec in self.prog[e]:
                for d in rec["deps"]:
                    if d[0] == "eng":
                        assert self.prog[d[1]][d[2]]["kind"] == "op", (d, self.prog[d[1]][d[2]])
                        need[d[1]].add(d[2])
        val = {e: {idx: r + 1 for r, idx in enumerate(sorted(need[e]))} for e in ENGS}
        sems = {e: nc.alloc_semaphore("s_" + e) for e in ENGS}
        dsems = {s: nc.alloc_semaphore("d_%d" % i) for i, s in enumerate(self.dma_cnt)}
        prog = self.prog

        def run(e, eng):
            seen = {}
            for idx, rec in enumerate(prog[e]):
                for d in rec["deps"]:
                    if d[0] == "eng":
                        if d[1] == e and e == "pe":
                            continue
                        key = ("e", d[1])
                        s = sems[d[1]]
                        v = val[d[1]][d[2]]
                    else:
                        key = ("d", d[1])
                        s = dsems[d[1]]
                        v = 16 * d[2]
                    if seen.get(key, 0) >= v:
                        continue
                    eng.wait_ge(s, v)
                    seen[key] = v
                if rec["fn"] is None:
                    continue
                ins = rec["fn"](eng)
                if rec["kind"] == "dma":
                    ins.then_inc(dsems[rec["sem"]], 16)
                elif idx in val[e]:
                    ins.then_inc(sems[e], 1)

        with nc.Block() as blk:
            @blk.tensor
            def _(eng):
                run("pe", eng)

            @blk.scalar
            def _(eng):
                run("act", eng)

            @blk.vector
            def _(eng):
                run("dve", eng)

            @blk.gpsimd
            def _(eng):
                run("pool", eng)

            @blk.sync
            def _(eng):
                run("sp", eng)


def blocks_A():
    groups = []
    for t in range(16):
        blks = []
        if t == 0:
            blks.append(dict(k=(0, 1, 64), bt=5, vt=0))
        else:
            blks.append(dict(k=(128 * t - 64, 1, 128), bt=0, vt=t))
        if t == 15:
            blks.append(dict(k=(128 * t + 64, 1, 64), bt=1, vt=16))
        else:
            blks.append(dict(k=(128 * t + 64, 1, 128), bt=1, vt=t + 1))
        groups.append(dict(branch=1, obank=t // 4, ocol=128 * (t % 4), q=(128 * t, 1), blocks=blks))
    for c in range(4):
        for t in range(4):
            blks = []
            if t == 0:
                blks.append(dict(k=(c, 4, 64), bt=6, vt=17 + 5 * c))
            else:
                blks.append(dict(k=(c + 4 * (128 * t - 64), 4, 128), bt=2, vt=17 + 5 * c + t))
            if t == 3:
                blks.append(dict(k=(c + 4 * (128 * t + 64), 4, 64), bt=3, vt=17 + 5 * c + 4))
            else:
                blks.append(dict(k=(c + 4 * (128 * t + 64), 4, 128), bt=3, vt=17 + 5 * c + t + 1))
            groups.append(dict(branch=2, obank=c, ocol=128 * t, q=(c + 4 * 128 * t, 4), blocks=blks))
    for r in range(16):
        groups.append(dict(branch=3, obank=r // 4, ocol=128 * (r % 4), q=(r, 16),
                           blocks=[dict(k=(r, 16, 128), bt=4, vt=37 + r)]))
    return groups


def vtiles_A():
    tiles = []
    for j in range(17):
        if j == 0:
            tiles.append((0, (0, 1, 64)))
        elif j == 16:
            tiles.append((16, (1984, 1, 64)))
        else:
            tiles.append((j, (128 * j - 64, 1, 128)))
    for c in range(4):
        for j in range(5):
            if j == 0:
                tiles.append((17 + 5 * c, (c, 4, 64)))
            elif j == 4:
                tiles.append((17 + 5 * c + 4, (c + 4 * 448, 4, 64)))
            else:
                tiles.append((17 + 5 * c + j, (c + 4 * (128 * j - 64), 4, 128)))
    for r in range(16):
        tiles.append((37 + r, (r, 16, 128)))
    return tiles


def _rs(rho):
    return min(max(rho - 4, 0), 24)


def blocks_B():
    groups = []
    for i in range(16):
        blks = []
        for kt in range(16):
            pat = []
            for a in (0, 1):
                for b in (0, 1):
                    kr, qr = 2 * kt + a, 2 * i + b
                    pat.append(_rs(qr) <= kr <= _rs(qr) + 7)
            if not any(pat):
                continue
            dl = kt - i
            interior = [(-4 <= 2 * dl + a - b <= 3) for a in (0, 1) for b in (0, 1)]
            if pat == interior and -2 <= dl <= 2:
                bt = dl + 2
            elif all(pat) and dl == 2:
                bt = 5
            elif all(pat) and dl == 3:
                bt = 6
            elif all(pat) and dl == -3:
                bt = 7
            elif all(pat) and dl == -2:
                bt = 8
            else:
                raise AssertionError((i, kt, pat))
            blks.append(dict(k=(128 * kt, 1, 128), bt=bt, vt=kt))
        groups.append(dict(branch=0, obank=i // 4, ocol=128 * (i % 4), q=(128 * i, 1), blocks=blks))
    return groups


def vtiles_B():
    return [(j, (128 * j, 1, 128)) for j in range(16)]


def _t5_buckets(rel):
    nb, maxd = 32, 1024
    half = nb // 2
    max_exact = half // 2
    n = np.abs(rel)
    large = max_exact + (np.log(np.maximum(n, 1) / max_exact)
                         / np.log(maxd / max_exact) * (half - max_exact)).astype(np.int64)
    large = np.minimum(large, half - 1)
    return ((rel > 0).astype(np.int64) * half + np.where(n < max_exact, n, large)).astype(np.int64)


def host_biasA(t5):
    k = np.arange(128)[:, None]
    q = np.arange(128)[None, :]
    specs = []
    for d in (1, 4):
        specs.append((k - 64 - q, (k >= q), d))
        specs.append((k + 64 - q, (k <= q), d))
    specs.append((k - q, (np.abs(k - q) <= 64), 16))
    for d in (1, 4):
        specs.append((k - q, ((k - q) >= -64) & (k < 64), d))
    out = np.full((16, 128, NTA, 128), NEG, np.float32)
    for ti, (rel, valid, d) in enumerate(specs):
        bk = _t5_buckets(rel * d)
        vals = t5[bk]
        for h in range(16):
            out[h, :, ti, :] = np.where(valid, vals[:, :, h], np.float32(NEG))
    out = out.reshape(8, 2, 128, NTA, 128).transpose(0, 2, 1, 3, 4).reshape(8, 128, 2 * NTA * 128)
    return np.ascontiguousarray(out)


def host_biasB(rpb):
    a = (np.arange(128) // 64)[:, None]
    cp = (np.arange(128) % 64)[:, None]
    b = (np.arange(128) // 64)[None, :]
    c = (np.arange(128) % 64)[None, :]
    qs = np.clip(c - 8, 0, 48)
    colv = (cp >= qs) & (cp < qs + 16)
    dcol = np.clip(cp - c, -15, 15) + 15
    variants = [(-2, "int"), (-1, "int"), (0, "int"), (1, "int"), (2, "int"),
                (2, "all"), (3, "all"), (-3, "all"), (-2, "all")]
    out = np.full((16, 128, NTB, 128), NEG, np.float32)
    for ti, (dl, kind) in enumerate(variants):
        dr = 2 * dl + a - b
        rowv = ((dr >= -4) & (dr <= 3)) if kind == "int" else np.ones_like(dr, bool)
        ri = np.clip(dr + 7, 0, 14)
        valid = rowv & colv
        for h in range(16):
            out[h, :, ti, :] = np.where(valid, rpb[h][ri, dcol], np.float32(NEG))
    out = out.reshape(8, 2, 128, NTB, 128).transpose(0, 2, 1, 3, 4).reshape(8, 128, 2 * NTB * 128)
    return np.ascontiguousarray(out)


def host_w_in(w_in):
    w = w_in.reshape(16, 128, 2, 4, 8, 128)
    w = w.transpose(2, 4, 1, 0, 3, 5)
    return np.ascontiguousarray(w.reshape(16, 128, 16 * 512))


def host_w_out(w_out):
    w = w_out.reshape(16, 128, 2048).transpose(1, 0, 2)
    return np.ascontiguousarray(w.reshape(128, 16 * 2048))


def host_consts():
    c = np.zeros((128, 128 + 192 + 64), np.float32)
    c[:, 320:384] = 1.0
    c[np.arange(128), np.arange(128)] = 1.0
    c[np.arange(64), 128 + 64 + np.arange(64)] = 1.0
    return c


def build(pairs=tuple(range(16)), debug_y=False):
    npairs = len(pairs)
    nc = bass.Bass("TRN2", target_bir_lowering=False)
    x = nc.dram_tensor("x", [S, D], F32, kind="ExternalInput").ap()
    w_in = nc.dram_tensor("w_in", [16, 128, 8192], F32, kind="ExternalInput").ap()
    w_out = nc.dram_tensor("w_out", [128, 16 * 2048], F32, kind="ExternalInput").ap()
    biasA = nc.dram_tensor("biasA", [8, 128, 2 * NTA * 128], F32, kind="ExternalInput").ap()
    biasB = nc.dram_tensor("biasB", [8, 128, 2 * NTB * 128], F32, kind="ExternalInput").ap()
    lng = nc.dram_tensor("lng", [128, D], F32, kind="ExternalInput").ap()
    lnb = nc.dram_tensor("lnb", [128, D], F32, kind="ExternalInput").ap()
    consts = nc.dram_tensor("consts", [128, 384], F32, kind="ExternalInput").ap()
    out = nc.dram_tensor("out", [S, D], F32, kind="ExternalOutput").ap()
    yscr_t = nc.dram_tensor("yscr", [16, 128, 16, 128], BF16,
                            kind=("ExternalOutput" if debug_y else "Internal"))
    yscr = yscr_t.ap()

    sb = nc.alloc_sbuf_tensor

    def alias(name, shape, dtype, base, off=0):
        return nc.alloc_sbuf_tensor_at(name, shape, dtype, offset=nc.lookup_mloc(base).addr + off)

    BIG = sb("BIG", [128, 16, 2048], BF16)
    W = [sb("W%d" % i, [128, 8192], BF16) for i in range(2)]
    ACC = [sb("ACC%d" % i, [128, 2048], F32) for i in range(2)]
    G = [sb("G%d" % i, [128, 2048], BF16) for i in range(2)]
    QT = [sb("QT%d" % i, [128, 2048], BF16) for i in range(2)]
    KT = [sb("KT%d" % i, [128, 2048], BF16) for i in range(2)]
    VT = [sb("VT%d" % i, [128, 2048], BF16) for i in range(2)]
    Vt = [sb("Vt%d" % i, [128, 53, 2, 65], BF16) for i in range(2)]
    biasT = [sb("biasT%d" % i, [128, 2 * NTB * 128], F32) for i in range(2)]
    Sb = [sb("Sb%d" % i, [128, 512], F32) for i in range(2)]
    PT = [sb("PT%d" % i, [128, 512], BF16) for i in range(5)]
    yn = [sb("yn%d" % i, [128, 512], BF16) for i in range(2)]
    ybuf = [sb("ybuf%d" % i, [128, 512], BF16) for i in range(2)]
    cst = sb("cst", [128, 384], BF16)
    hl = sb("hl", [128, 2, 2, 512], BF16)
    ones32 = sb("ones32", [128, 64], F32)
    stt = [sb("stt%d" % i, [128, 24], F32) for i in range(2)]
    mv = [sb("mv%d" % i, [128, 4], F32) for i in range(2)]
    xin = [alias("xin0", [128, 2048], BF16, ACC[0]), alias("xin1", [128, 2048], BF16, ACC[0], 4096)]
    xres = [alias("xres0", [128, 2048], F32, QT[0]), alias("xres1", [128, 2048], F32, G[0])]
    rbuf = [alias("rbuf0", [128, 2048], F32, KT[0]), alias("rbuf1", [128, 2048], F32, VT[0])]
    yTt = [alias("yTt0", [128, 16, 128], BF16, Vt[0]), alias("yTt1", [128, 16, 128], BF16, Vt[0], 4096)]

    ad = lambda t: nc.lookup_mloc(t).addr
    assert ad(G[1]) == ad(G[0]) + 4096 and ad(QT[1]) == ad(QT[0]) + 4096 and ad(KT[1]) == ad(KT[0]) + 4096 and ad(VT[1]) == ad(VT[0]) + 4096
    P = [nc.alloc_psum_tensor("P%d" % i, [128, 512], F32) for i in range(2)]
    SP = [nc.alloc_psum_tensor("S%d" % i, [128, 512], F32) for i in range(4)]
    OP = [nc.alloc_psum_tensor("O%d" % i, [128, 512], F32) for i in range(2)]
    TP = [P[i][:, :].bitcast(BF16) for i in range(2)]

    ident = cst[:, 0:128]
    E_lo = cst[0:64, 192:320]
    E_sh = cst[0:64, 128:256]

    def AP(t, off, dims):
        return bass.AP(t, off, [list(d) for d in dims])

    def pstride(t):
        return t[:].ap[0][0]

    def col_ap(t, prow, nrow, start, step, n):
        return AP(t, prow * pstride(t) + start, [[pstride(t), nrow], [step, n]])

    sc = Sched()
    cnt = dict(p=0, s=0, o=0, t=0, sb=0, pt=0, yb=0)

    def nxt(k, n):
        v = cnt[k] % n
        cnt[k] += 1
        return v

    sc.dma("pool", lambda e: e.dma_start(out=cst[:], in_=consts), writes=["cst"], sem="cst")
    sc.op("dve", lambda e: e.memset(ones32[:], 1.0), writes=["ones32"])
    for b in range(2):
        sc.op("dve", lambda e, b=b: e.memset(Vt[b][:, :, :, 64:65], 1.0), writes=[("Vt", b)])

    def load_w(p, sl):
        for c4 in range(4):
            sc.dma("pool", lambda e, sl=sl, p=p, c4=c4: e.dma_start(
                out=W[sl][:, 2048 * c4:2048 * c4 + 2048], in_=w_in[p, :, 2048 * c4:2048 * c4 + 2048]),
                writes=[("W", sl)], sem="W%d" % sl)

    grpsA = blocks_A()
    grpsB = blocks_B()
    vtA = vtiles_A()
    vtB = vtiles_B()

    def proj_group(pi, oi, ck):
        p = pairs[pi]
        wsl = pi % 2
        bsl = pi % 2
        pb = nxt("p", 2)
        for dc in range(16):
            sc.op("pe", lambda e, pb=pb, wsl=wsl, dc=dc, oi=oi, ck=ck: e.matmul(
                P[pb][:, :], W[wsl][:, 512 * dc + 128 * oi:512 * dc + 128 * oi + 128],
                BIG[:, dc, 512 * ck:512 * ck + 512], start=(dc == 0), stop=(dc == 15)),
                reads=[("W", wsl), ("xT", 0, ck), ("xT", 1, ck)], writes=[("P", pb)])
        cs = slice(512 * ck, 512 * ck + 512)
        if oi == 0:
            sc.op("act", lambda e, pb=pb, cs=cs: e.copy(QT[bsl][:, cs], P[pb][:, :]),
                  reads=[("P", pb)], writes=[("QT", bsl, ck)])
        elif oi == 1:
            sc.op("act", lambda e, pb=pb, cs=cs: e.copy(KT[bsl][:, cs], P[pb][:, :]),
                  reads=[("P", pb)], writes=[("KT", bsl, ck)])
        elif oi == 2:
            sc.op("dve", lambda e, pb=pb, cs=cs: e.tensor_copy(VT[bsl][:, cs], P[pb][:, :]),
                  reads=[("P", pb)], writes=[("VT", bsl, ck)])
        else:
            sc.op("act", lambda e, pb=pb, cs=cs: e.activation(G[bsl][:, cs], P[pb][:, :], AF.Silu),
                  reads=[("P", pb)], writes=[("G", bsl, ck)])

    def units_P(pi):
        p = pairs[pi]
        isA = p < 8
        bsl = pi % 2
        units = []

        def u_first():
            if isA:
                sc.dma("sp", lambda e: e.dma_start(out=biasT[bsl][:, 0:2 * NTA * 128], in_=biasA[p]),
                       writes=[("bias", bsl)], sem="bias%d" % bsl)
            else:
                sc.dma("sp", lambda e: e.dma_start(out=biasT[bsl][:, 0:2 * NTB * 128], in_=biasB[p - 8]),
                       writes=[("bias", bsl)], sem="bias%d" % bsl)
        units.append(u_first)
        if pi == 0:
            pass
        else:
            for oi in (2, 1, 0, 3):
                for ck in range(4):
                    units.append(lambda oi=oi, ck=ck: proj_group(pi, oi, ck))
        vts = vtA if isA else vtB
        full = [v for v in vts if v[1][2] == 128]
        edge = [v for v in vts if v[1][2] == 64]
        VTk = [("VT", bsl, c) for c in range(4)]
        i0 = 0
        runs = []
        while i0 < len(full):
            run = [full[i0]]
            while len(run) < 8 and i0 + len(run) < len(full) and full[i0 + len(run)][0] == run[-1][0] + 1:
                run.append(full[i0 + len(run)])
            i0 += len(run)
            runs.append(run)

        def u_run(run):
            tb = nxt("p", 2)
            for j, (ti, (st, sp_, nk)) in enumerate(run):
                sc.op("pe", lambda e, tb=tb, j=j, st=st, sp_=sp_: e.transpose(
                    TP[tb][:, 128 * j:128 * j + 128], col_ap(VT[bsl], 0, 128, st, sp_, 128), ident),
                    reads=VTk + ["cst"], writes=[("P", tb)])
            n = len(run)
            t0 = run[0][0]
            dst = Vt[bsl][:, t0:t0 + n, :, 0:64]
            src = TP[tb][:, 0:128 * n].rearrange("p (a h c) -> p a h c", a=n, h=2)
            sc.op("act", lambda e, dst=dst, src=src: e.copy(dst, src),
                  reads=[("P", tb)], writes=[("Vt", bsl)])

        def u_edges(es):
            tb = nxt("p", 2)
            for j, (ti, (st, sp_, nk)) in enumerate(es):
                sc.op("pe", lambda e, tb=tb, j=j, st=st, sp_=sp_: e.transpose(
                    TP[tb][0:64, 128 * j:128 * j + 128], col_ap(VT[bsl], 0, 128, st, sp_, 64), ident),
                    reads=VTk + ["cst"], writes=[("P", tb)])
            for j, (ti, _) in enumerate(es):
                dst = Vt[bsl][0:64, ti, :, 0:64]
                src = TP[tb][0:64, 128 * j:128 * j + 128].rearrange("p (h c) -> p h c", h=2)
                sc.op("act", lambda e, dst=dst, src=src: e.copy(dst, src),
                      reads=[("P", tb)], writes=[("Vt", bsl)])
        for run in runs:
            units.append(lambda run=run: u_run(run))
        for i0 in range(0, len(edge), 8):
            units.append(lambda es=edge[i0:i0 + 8]: u_edges(es))
        if pi + 1 < npairs:
            units.insert(1, lambda: load_w(pairs[pi + 1], (pi + 1) % 2))
        return units

    def units_G(pi):
        return [lambda ck=ck: proj_group(pi, 3, ck) for ck in range(4)]

    def units_T(pi):
        p = pairs[pi]
        isA = p < 8
        bsl = pi % 2
        NT = NTA if isA else NTB
        grps = grpsA if isA else grpsB
        QTk = [("QT", bsl, c) for c in range(4)]
        KTk = [("KT", bsl, c) for c in range(4)]
        units = []
        state = {}
        allA, allB = [], []
        for e_ in range(2):
            prow = 64 * e_
            acc = ACC[e_]
            ps_acc = pstride(acc)
            order = sorted(range(len(grps)), key=lambda gi: (grps[gi]["branch"], grps[gi]["obank"], grps[gi]["ocol"]))
            stream = []
            for gi in order:
                g = grps[gi]
                nb = len(g["blocks"])
                for bi, b in enumerate(g["blocks"]):
                    stream.append(dict(g=g, b=b, first=(bi == 0), last=(bi == nb - 1)))
            packs = []
            cur = []
            for it in stream:
                if it["b"]["k"][2] == 64:
                    if cur:
                        packs.append(cur)
                        cur = []
                    packs.append([it])
                else:
                    cur.append(it)
                    if len(cur) == 4:
                        packs.append(cur)
                        cur = []
            if cur:
                packs.append(cur)

            def flush_obank(key, ob, e_=e_, acc=acc, ps_acc=ps_acc):
                br, obk = key
                src = OP[ob][0:65, :]
                if br in (0, 1):
                    dst = acc[0:65, 512 * obk:512 * obk + 512]
                    sc.op("act", lambda e, dst=dst, src=src: e.copy(dst, src),
                          reads=[("O", ob)], writes=[("ACC", e_)])
                else:
                    if br == 2:
                        view = AP(acc, obk, [[ps_acc, 65], [512, 4], [4, 128]])
                    else:
                        view = AP(acc, 4 * obk, [[ps_acc, 65], [1, 4], [16, 128]])
                    src3 = src.rearrange("p (a b) -> p a b", a=4)
                    sc.op("dve", lambda e, view=view, src3=src3: e.tensor_tensor(view, src3, view, ALU.add),
                          reads=[("O", ob), ("ACC", e_)], writes=[("ACC", e_)])

            def u_packA(pk, e_=e_, prow=prow):
                nk = pk[0]["b"]["k"][2]
                nblk = len(pk)
                sbk = nxt("s", 4)
                for j, it in enumerate(pk):
                    ks, kp, _ = it["b"]["k"]
                    qs, qp = it["g"]["q"]
                    sc.op("pe", lambda e, sbk=sbk, j=j, ks=ks, kp=kp, qs=qs, qp=qp: e.matmul(
                        SP[sbk][0:nk, 128 * j:128 * j + 128],
                        col_ap(KT[bsl], prow, 64, ks, kp, nk), col_ap(QT[bsl], prow, 64, qs, qp, 128),
                        start=True, stop=True),
                        reads=QTk + KTk, writes=[("S", sbk)])
                sbs = nxt("sb", 2)
                bts = [it["b"]["bt"] for it in pk]
                j0 = 0
                for r in (1, 2):
                    if nblk % r == 0 and nblk // r >= 2:
                        run = bts[:r]
                        if all(run[i] == run[0] + i for i in range(r)) and bts == run * (nblk // r):
                            nrep = nblk // r
                            boff = (e_ * NT + run[0]) * 128
                            in1 = AP(biasT[bsl], boff, [[pstride(biasT[bsl]), nk], [0, nrep], [1, r * 128]])
                            o3 = Sb[sbs][0:nk, 0:128 * nblk].rearrange("p (a b) -> p a b", a=nrep)
                            i3 = SP[sbk][0:nk, 0:128 * nblk].rearrange("p (a b) -> p a b", a=nrep)
                            sc.op("dve", lambda e, o3=o3, i3=i3, in1=in1: e.scalar_tensor_tensor(
                                o3, i3, SCALE, in1, ALU.mult, ALU.add),
                                reads=[("S", sbk), ("bias", bsl)], writes=[("Sb", sbs)])
                            j0 = nblk
                            break
                while j0 < nblk:
                    j1 = j0 + 1
                    while j1 < nblk and bts[j1] == bts[j1 - 1] + 1:
                        j1 += 1
                    boff = (e_ * NT + bts[j0]) * 128
                    n = 128 * (j1 - j0)
                    sc.op("dve", lambda e, sbs=sbs, sbk=sbk, j0=j0, n=n, boff=boff: e.scalar_tensor_tensor(
                        Sb[sbs][0:nk, 128 * j0:128 * j0 + n], SP[sbk][0:nk, 128 * j0:128 * j0 + n], SCALE,
                        biasT[bsl][0:nk, boff:boff + n], ALU.mult, ALU.add),
                        reads=[("S", sbk), ("bias", bsl)], writes=[("Sb", sbs)])
                    j0 = j1
                pts = nxt("pt", 5)
                sc.op("act", lambda e, pts=pts, sbs=sbs: e.activation(
                    PT[pts][0:nk, 0:128 * nblk], Sb[sbs][0:nk, 0:128 * nblk], AF.Exp),
                    reads=[("Sb", sbs)], writes=[("PT", pts)])
                pk[0]["pts"] = pts

            def u_packB(pk, e_=e_, flush_obank=flush_obank):
                nk = pk[0]["b"]["k"][2]
                pts = pk[0]["pts"]
                for j, it in enumerate(pk):
                    g = it["g"]
                    key = (e_, g["branch"], g["obank"])
                    if key != state.get("cur"):
                        if state.get("cur") is not None:
                            state["flush"](state["cur"][1:], state["ob"])
                        state["cur"] = key
                        state["ob"] = nxt("o", 2)
                        state["flush"] = flush_obank
                    ob = state["ob"]
                    vt = it["b"]["vt"]
                    oc = g["ocol"]
                    sc.op("pe", lambda e, ob=ob, oc=oc, vt=vt, pts=pts, j=j, f=it["first"], l=it["last"]: e.matmul(
                        OP[ob][0:65, oc:oc + 128], Vt[bsl][0:nk, vt, e_, :], PT[pts][0:nk, 128 * j:128 * j + 128],
                        start=f, stop=l),
                        reads=[("Vt", bsl), ("PT", pts)], writes=[("O", ob)])

            for pk in packs:
                pk = [dict(it) for it in pk]
                allA.append(lambda pk=pk, f=u_packA: f(pk))
                allB.append(lambda pk=pk, f=u_packB: f(pk))

        LAG = 4
        for k in range(len(allA)):
            units.append(allA[k])
            if k >= LAG:
                units.append(allB[k - LAG])
        for k in range(max(0, len(allA) - LAG), len(allA)):
            units.append(allB[k])

        def u_flush_last():
            state["flush"](state["cur"][1:], state["ob"])
            state["cur"] = None
        units.append(u_flush_last)

        def fS1(ck):
            cs = slice(512 * ck, 512 * ck + 512)
            for e_ in range(2):
                sc.op("act", lambda e, e_=e_: e.activation(ACC[e_][64:65, cs], ACC[e_][64:65, cs], AF.Ln),
                      reads=[("ACC", e_)], writes=[("ACCd", e_, ck)])
                sc.op("act", lambda e, e_=e_: e.activation(ACC[e_][64:65, cs], ACC[e_][64:65, cs], AF.Exp, scale=-1.0),
                      reads=[("ACCd", e_, ck)], writes=[("ACCd", e_, ck)])
                sc.op("act", lambda e, e_=e_: e.copy(hl[64:65, 0, e_, :], ACC[e_][64:65, cs]),
                      reads=[("ACCd", e_, ck)], writes=[("hl", 0, e_)])
            for e_ in range(2):
                sc.op("pool", lambda e, e_=e_: e.tensor_tensor(
                    hl[64:65, 1, e_, :], ACC[e_][64:65, cs], hl[64:65, 0, e_, :], ALU.subtract),
                    reads=[("ACCd", e_, ck), ("hl", 0, e_)], writes=[("hl", 1, e_)])

        def fS2(ck):
            cs = slice(512 * ck, 512 * ck + 512)
            pbs = []
            for e_ in range(2):
                pb = nxt("p", 2)
                pbs.append(pb)
                sc.op("pe", lambda e, pb=pb, e_=e_: e.matmul(
                    P[pb][0:64, :], cst[64:65, 320:384], hl[64:65, 0, e_, :], start=True, stop=False),
                    reads=["cst", ("hl", 0, e_)], writes=[("P", pb)])
                sc.op("pe", lambda e, pb=pb, e_=e_: e.matmul(
                    P[pb][0:64, :], cst[64:65, 320:384], hl[64:65, 1, e_, :], start=False, stop=True),
                    reads=["cst", ("hl", 1, e_)], writes=[("P", pb)])
            for e_ in range(2):
                pb = pbs[e_]
                sc.op("dve", lambda e, pb=pb, e_=e_: e.tensor_tensor(
                    yn[e_][0:64, :], ACC[e_][0:64, cs], P[pb][0:64, :], ALU.mult),
                    reads=[("P", pb), ("ACC", e_)], writes=[("yn", e_)])

        def fS3(ck):
            cs = slice(512 * ck, 512 * ck + 512)
            pb = nxt("p", 2)
            sc.op("pe", lambda e, pb=pb: e.matmul(P[pb][:, :], E_lo, yn[0][0:64, :], start=True, stop=False),
                  reads=["cst", ("yn", 0)], writes=[("P", pb)])
            sc.op("pe", lambda e, pb=pb: e.matmul(P[pb][:, :], E_sh, yn[1][0:64, :], start=False, stop=True),
                  reads=["cst", ("yn", 1)], writes=[("P", pb)])
            ys = nxt("yb", 2)
            sc.op("dve", lambda e, pb=pb, ys=ys: e.tensor_tensor(
                ybuf[ys][:, :], P[pb][:, :], G[bsl][:, cs], ALU.mult),
                reads=[("P", pb), ("G", bsl, ck)], writes=[("ybuf", ys)])
            dst = yscr[4 * ck:4 * ck + 4, :, p, :].rearrange("i e t -> e i t")
            src = ybuf[ys][:, :].rearrange("e (i t) -> e i t", i=4)
            sc.dma("sp", lambda e, dst=dst, src=src: e.dma_start(out=dst, in_=src),
                   reads=[("ybuf", ys)], writes=["yscr"], sem="yst%d" % ys)

        seq = [("1", 0), ("2", 0), ("1", 1), ("3", 0), ("2", 1), ("1", 2), ("3", 1), ("2", 2), ("1", 3),
               ("3", 2), ("2", 3), ("3", 3)]
        fmap = {"1": fS1, "2": fS2, "3": fS3}
        for kind, ck in seq:
            units.append(lambda kind=kind, ck=ck: fmap[kind](ck))
        return units

    def phase0():
        for i in range(16):
            sl = i % 2
            sc.dma("pool", lambda e, sl=sl, i=i: e.dma_start(out=xin[sl][:], in_=x[128 * i:128 * i + 128, :]),
                   writes=[("xin", sl)], sem="xin%d" % sl)
            if i == 1 and npairs > 0:
                load_w(pairs[0], 0)
            for g in range(2):
                tb = nxt("p", 2)
                for j in range(8):
                    dc = 8 * g + j
                    sc.op("pe", lambda e, tb=tb, j=j, sl=sl, dc=dc: e.transpose(
                        TP[tb][:, 128 * j:128 * j + 128], xin[sl][:, 128 * dc:128 * dc + 128], ident),
                        reads=[("xin", sl), "cst"], writes=[("P", tb)])
                dst = BIG[:, 8 * g:8 * g + 8, 128 * i:128 * i + 128]
                src = TP[tb][:, :].rearrange("p (a b) -> p a b", a=8)
                if g == 0:
                    sc.op("act", lambda e, dst=dst, src=src: e.copy(dst, src),
                          reads=[("P", tb)], writes=[("xT", g, i // 4)])
                else:
                    sc.op("dve", lambda e, dst=dst, src=src: e.tensor_copy(dst, src),
                          reads=[("P", tb)], writes=[("xT", g, i // 4)])
            if i % 4 == 3 and npairs > 0:
                for oi in (2, 1, 0, 3):
                    proj_group(0, oi, i // 4)


    phase0()
    if npairs > 0:
        for u in units_P(0):
            u()
    for pi in range(npairs):
        tu = units_T(pi)
        pu = units_P(pi + 1) if pi + 1 < npairs else []
        nT, nP = len(tu), len(pu)
        span = max(1, int(nT * 0.85))
        ip = 0
        for it_, u in enumerate(tu):
            u()
            while ip < nP and (ip + 1) * span <= (it_ + 1) * nP:
                pu[ip]()
                ip += 1
        while ip < nP:
            pu[ip]()
            ip += 1
        if pi + 1 == npairs - 1 or npairs == 1:
            for c8 in range(8):
                sc.dma("pool", lambda e, c8=c8: e.dma_start(
                    out=BIG[:, 2 * c8:2 * c8 + 2, :],
                    in_=w_out[:, 4096 * c8:4096 * c8 + 4096].rearrange("p (a b) -> p a b", a=2)),
                    writes=[("WO", c8)] + [("xT", g_, c_) for g_ in range(2) for c_ in range(4)], sem="WO%d" % c8)

    sc.barrier()
    WO = BIG
    WOk = [("WO", c8) for c8 in range(8)]
    gain = ACC[0]
    lbias = ACC[1]
    sc.dma("sp", lambda e: e.dma_start(out=gain[:], in_=lng), writes=["gain"], sem="gain")
    sc.dma("sp", lambda e: e.dma_start(out=lbias[:], in_=lnb), writes=["lbias"], sem="lbias")
    def p2_loads(i):
        ysl = i % 2
        xsl = i % 2
        sc.dma("sp", lambda e, ysl=ysl, i=i: e.dma_start(out=yTt[ysl][:], in_=yscr[i]),
               reads=["yscr"], writes=[("yTt", ysl)], sem="yTt%d" % ysl)
        sc.dma("sp", lambda e, i=i, xsl=xsl: e.dma_start(out=xres[xsl][:], in_=x[128 * i:128 * i + 128, :]),
               writes=[("xres", xsl)], sem="xres%d" % xsl)

    p2_loads(0)
    pend_tail = []
    for i in range(16):
        ysl = i % 2
        rsl = i % 2
        xsl = i % 2
        if i + 1 < 16:
            p2_loads(i + 1)
        for n in range(4):
            cs = slice(512 * n, 512 * n + 512)
            pb = nxt("p", 2)
            for ec in range(16):
                sc.op("pe", lambda e, pb=pb, ysl=ysl, ec=ec, cs=cs: e.matmul(
                    P[pb][:, :], yTt[ysl][:, ec, :], WO[:, ec, cs], start=(ec == 0), stop=(ec == 15)),
                    reads=[("yTt", ysl), ("WO", ec // 2)], writes=[("P", pb)])
            sc.op("dve", lambda e, pb=pb, rsl=rsl, cs=cs, xsl=xsl: e.scalar_tensor_tensor(
                rbuf[rsl][:, cs], xres[xsl][:, cs], ALPHA, P[pb][:, :], ALU.mult, ALU.add),
                reads=[("P", pb), ("xres", xsl)], writes=[("rbuf", rsl, n)])
            sc.op("dve", lambda e, rsl=rsl, n=n, cs=cs: e.bn_stats(stt[rsl][:, 6 * n:6 * n + 6], rbuf[rsl][:, cs]),
                  reads=[("rbuf", rsl, n)], writes=[("stt", rsl, n)])
            if pend_tail:
                pend_tail.pop(0)()
                if n == 3:
                    while pend_tail:
                        pend_tail.pop(0)()
        rk = [("rbuf", rsl, n) for n in range(4)]
        sc.op("dve", lambda e, rsl=rsl: e.bn_aggr(mv[rsl][:, 0:2], stt[rsl][:, 0:24]),
              reads=[("stt", rsl, n) for n in range(4)], writes=[("mv", rsl)])
        sc.op("dve", lambda e, rsl=rsl: e.tensor_scalar(
            mv[rsl][:, 2:3], mv[rsl][:, 1:2], LN_EPS, None, ALU.add),
            reads=[("mv", rsl)], writes=[("mv2", rsl)])
        sc.op("act", lambda e, rsl=rsl: e.sqrt(mv[rsl][:, 2:3], mv[rsl][:, 2:3]),
              reads=[("mv2", rsl)], writes=[("mv2", rsl)])
        sc.op("dve", lambda e, rsl=rsl: e.reciprocal(mv[rsl][:, 2:3], mv[rsl][:, 2:3]),
              reads=[("mv2", rsl)], writes=[("mv2", rsl)])
        sc.op("dve", lambda e, rsl=rsl: e.scalar_tensor_tensor(
            mv[rsl][:, 3:4], mv[rsl][:, 0:1], -1.0, mv[rsl][:, 2:3], ALU.mult, ALU.mult),
            reads=[("mv", rsl), ("mv2", rsl)], writes=[("mv3", rsl)])
        sc.op("act", lambda e, rsl=rsl: e.activation(
            rbuf[rsl][:, :], rbuf[rsl][:, :], AF.Identity, bias=mv[rsl][:, 3:4], scale=mv[rsl][:, 2:3]),
            reads=rk + [("mv2", rsl), ("mv3", rsl)], writes=rk)

        def mk_tail(rsl=rsl, i=i, rk=rk):
            ops = []
            for hh in range(2):
                hs = slice(1024 * hh, 1024 * hh + 1024)
                ops.append(lambda hs=hs, hh=hh: sc.op("dve", lambda e: e.tensor_tensor(
                    rbuf[rsl][:, hs], rbuf[rsl][:, hs], gain[:, hs], ALU.mult),
                    reads=rk + ["gain"], writes=[("rbh", rsl, hh)]))
                ops.append(lambda hs=hs, hh=hh: sc.op("dve", lambda e: e.tensor_tensor(
                    rbuf[rsl][:, hs], rbuf[rsl][:, hs], lbias[:, hs], ALU.add),
                    reads=[("rbh", rsl, hh), "lbias"], writes=[("rbh", rsl, hh)]))
            ops.append(lambda: sc.dma("sp", lambda e: e.dma_start(out=out[128 * i:128 * i + 128, :], in_=rbuf[rsl][:]),
                                      reads=[("rbh", rsl, 0), ("rbh", rsl, 1)], writes=rk + [("out", i)], sem="ost%d" % rsl))
            return ops
        pend_tail = mk_tail()
    for op_ in pend_tail:
        op_()
    sc.op("sp", None, reads=[("out", i) for i in range(16)] + (["yscr"] if debug_y else []))
    sc.emit(nc)
    return nc


_CACHE = {}


def prep_inputs(x, w_in, w_out, t5_bias, na_rpb, ln_gain, ln_bias):
    x = np.asarray(x, np.float32)
    shared = dict(
        w_in=host_w_in(np.asarray(w_in, np.float32)[0]),
        w_out=host_w_out(np.asarray(w_out, np.float32)[0]),
        biasA=host_biasA(np.asarray(t5_bias, np.float32)),
        biasB=host_biasB(np.asarray(na_rpb, np.float32)[0]),
        lng=np.ascontiguousarray(np.broadcast_to(np.asarray(ln_gain, np.float32)[0][None, :], (128, D))),
        lnb=np.ascontiguousarray(np.broadcast_to(np.asarray(ln_bias, np.float32)[0][None, :], (128, D))),
        consts=host_consts(),
    )
    in_maps = []
    for b in range(8):
        m = dict(shared)
        m["x"] = np.ascontiguousarray(x[b])
        in_maps.append(m)
    return in_maps


def kernel(x, w_in, w_out, t5_bias, na_rpb, ln_gain, ln_bias):
    in_maps = prep_inputs(x, w_in, w_out, t5_bias, na_rpb, ln_gain, ln_bias)
    if "nc" not in _CACHE:
        _CACHE["nc"] = build()
    nc = _CACHE["nc"]
    res = run_bass_kernel_spmd(nc, in_maps, core_ids=list(range(8)))
    outs = [np.asarray(r["out"], np.float32) for r in res.results]
    return np.stack(outs, axis=0)
```

```python
import numpy as np
import concourse.bass as bass
import concourse.mybir as mybir
from concourse.bass_utils import run_bass_kernel_spmd

F32 = mybir.dt.float32
BF16 = mybir.dt.bfloat16
AF = mybir.ActivationFunctionType
ALU = mybir.AluOpType

S = 2048
D = 2048
NEG = -30000.0
SCALE = 0.125
ALPHA = 2.0 ** 0.25
LN_EPS = 1e-5
NTA = 7
NTB = 9

ENGS = ("pe", "act", "dve", "pool", "sp")


class Sched:
    def __init__(self):
        self.prog = {e: [] for e in ENGS}
        self.lw = {}
        self.lr = {}
        self.dma_cnt = {}

    def _deps(self, reads, writes):
        deps = []
        for k in reads:
            if k in self.lw:
                deps.append(self.lw[k])
        for k in writes:
            if k in self.lw:
                deps.append(self.lw[k])
            for p in self.lr.get(k, {}).values():
                deps.append(p)
        return deps

    def op(self, eng, fn, reads=(), writes=()):
        deps = self._deps(reads, writes)
        idx = len(self.prog[eng])
        self.prog[eng].append(dict(fn=fn, deps=deps, kind="op"))
        me = ("eng", eng, idx)
        for k in reads:
            self.lr.setdefault(k, {})[eng] = me
        for k in writes:
            self.lw[k] = me
            self.lr[k] = {}
        return me

    def dma(self, queue, fn, reads=(), writes=(), sem=None):
        deps = self._deps(reads, writes)
        cnt = self.dma_cnt.get(sem, 0) + 1
        self.dma_cnt[sem] = cnt
        self.prog[queue].append(dict(fn=fn, deps=deps, kind="dma", sem=sem))
        me = ("dma", sem, cnt)
        for k in reads:
            self.lr.setdefault(k, {})["dma:" + sem] = me
        for k in writes:
            self.lw[k] = me
            self.lr[k] = {}
        return me

    def barrier(self):
        deps = []
        for e in ENGS:
            for idx in range(len(self.prog[e]) - 1, -1, -1):
                r = self.prog[e][idx]
                if r["kind"] == "op" and r["fn"] is not None:
                    deps.append(("eng", e, idx))
                    break
        for s, c in self.dma_cnt.items():
            deps.append(("dma", s, c))
        for e in ENGS:
            self.prog[e].append(dict(fn=None, deps=list(deps), kind="wait"))

    def emit(self, nc):
        need = {e: set() for e in ENGS}
        for e in ENGS:
            for rec in self.prog[e]:
                for d in rec["deps"]:
                    if d[0] == "eng":
                        assert self.prog[d[1]][d[2]]["kind"] == "op", (d, self.prog[d[1]][d[2]])
                        need[d[1]].add(d[2])
        val = {e: {idx: r + 1 for r, idx in enumerate(sorted(need[e]))} for e in ENGS}
        sems = {e: nc.alloc_semaphore("s_" + e) for e in ENGS}
        dsems = {s: nc.alloc_semaphore("d_%d" % i) for i, s in enumerate(self.dma_cnt)}
        prog = self.prog

        def run(e, eng):
            seen = {}
            for idx, rec in enumerate(prog[e]):
                for d in rec["deps"]:
                    if d[0] == "eng":
                        if d[1] == e and e == "pe":
                            continue
                        key = ("e", d[1])
                        s = sems[d[1]]
                        v = val[d[1]][d[2]]
                    else:
                        key = ("d", d[1])
                        s = dsems[d[1]]
                        v = 16 * d[2]
                    if seen.get(key, 0) >= v:
                        continue
                    eng.wait_ge(s, v)
                    seen[key] = v
                if rec["fn"] is None:
                    continue
                ins = rec["fn"](eng)
                if rec["kind"] == "dma":
                    ins.then_inc(dsems[rec["sem"]], 16)
                elif idx in val[e]:
                    ins.then_inc(sems[e], 1)

        with nc.Block() as blk:
            @blk.tensor
            def _(eng):
                run("pe", eng)

            @blk.scalar
            def _(eng):
                run("act", eng)

            @blk.vector
            def _(eng):
                run("dve", eng)

            @blk.gpsimd
            def _(eng):
                run("pool", eng)

            @blk.sync
            def _(eng):
                run("sp", eng)


def blocks_A():
    groups = []
    for t in range(16):
        blks = []
        if t == 0:
            blks.append(dict(k=(0, 1, 64), bt=5, vt=0))
        else:
            blks.append(dict(k=(128 * t - 64, 1, 128), bt=0, vt=t))
        if t == 15:
            blks.append(dict(k=(128 * t + 64, 1, 64), bt=1, vt=16))
        else:
            blks.append(dict(k=(128 * t + 64, 1, 128), bt=1, vt=t + 1))
        groups.append(dict(branch=1, obank=t // 4, ocol=128 * (t % 4), q=(128 * t, 1), blocks=blks))
    for c in range(4):
        for t in range(4):
            blks = []
            if t == 0:
                blks.append(dict(k=(c, 4, 64), bt=6, vt=17 + 5 * c))
            else:
                blks.append(dict(k=(c + 4 * (128 * t - 64), 4, 128), bt=2, vt=17 + 5 * c + t))
            if t == 3:
                blks.append(dict(k=(c + 4 * (128 * t + 64), 4, 64), bt=3, vt=17 + 5 * c + 4))
            else:
                blks.append(dict(k=(c + 4 * (128 * t + 64), 4, 128), bt=3, vt=17 + 5 * c + t + 1))
            groups.append(dict(branch=2, obank=c, ocol=128 * t, q=(c + 4 * 128 * t, 4), blocks=blks))
    for r in range(16):
        groups.append(dict(branch=3, obank=r // 4, ocol=128 * (r % 4), q=(r, 16),
                           blocks=[dict(k=(r, 16, 128), bt=4, vt=37 + r)]))
    return groups


def vtiles_A():
    tiles = []
    for j in range(17):
        if j == 0:
            tiles.append((0, (0, 1, 64)))
        elif j == 16:
            tiles.append((16, (1984, 1, 64)))
        else:
            tiles.append((j, (128 * j - 64, 1, 128)))
    for c in range(4):
        for j in range(5):
            if j == 0:
                tiles.append((17 + 5 * c, (c, 4, 64)))
            elif j == 4:
                tiles.append((17 + 5 * c + 4, (c + 4 * 448, 4, 64)))
            else:
                tiles.append((17 + 5 * c + j, (c + 4 * (128 * j - 64), 4, 128)))
    for r in range(16):
        tiles.append((37 + r, (r, 16, 128)))
    return tiles


def _rs(rho):
    return min(max(rho - 4, 0), 24)


def blocks_B():
    groups = []
    for i in range(16):
        blks = []
        for kt in range(16):
            pat = []
            for a in (0, 1):
                for b in (0, 1):
                    kr, qr = 2 * kt + a, 2 * i + b
                    pat.append(_rs(qr) <= kr <= _rs(qr) + 7)
            if not any(pat):
                continue
            dl = kt - i
            interior = [(-4 <= 2 * dl + a - b <= 3) for a in (0, 1) for b in (0, 1)]
            if pat == interior and -2 <= dl <= 2:
                bt = dl + 2
            elif all(pat) and dl == 2:
                bt = 5
            elif all(pat) and dl == 3:
                bt = 6
            elif all(pat) and dl == -3:
                bt = 7
            elif all(pat) and dl == -2:
                bt = 8
            else:
                raise AssertionError((i, kt, pat))
            blks.append(dict(k=(128 * kt, 1, 128), bt=bt, vt=kt))
        groups.append(dict(branch=0, obank=i // 4, ocol=128 * (i % 4), q=(128 * i, 1), blocks=blks))
    return groups


def vtiles_B():
    return [(j, (128 * j, 1, 128)) for j in range(16)]


def _t5_buckets(rel):
    nb, maxd = 32, 1024
    half = nb // 2
    max_exact = half // 2
    n = np.abs(rel)
    large = max_exact + (np.log(np.maximum(n, 1) / max_exact)
                         / np.log(maxd / max_exact) * (half - max_exact)).astype(np.int64)
    large = np.minimum(large, half - 1)
    return ((rel > 0).astype(np.int64) * half + np.where(n < max_exact, n, large)).astype(np.int64)


def host_biasA(t5):
    k = np.arange(128)[:, None]
    q = np.arange(128)[None, :]
    specs = []
    for d in (1, 4):
        specs.append((k - 64 - q, (k >= q), d))
        specs.append((k + 64 - q, (k <= q), d))
    specs.append((k - q, (np.abs(k - q) <= 64), 16))
    for d in (1, 4):
        specs.append((k - q, ((k - q) >= -64) & (k < 64), d))
    out = np.full((16, 128, NTA, 128), NEG, np.float32)
    for ti, (rel, valid, d) in enumerate(specs):
        bk = _t5_buckets(rel * d)
        vals = t5[bk]
        for h in range(16):
            out[h, :, ti, :] = np.where(valid, vals[:, :, h], np.float32(NEG))
    out = out.reshape(8, 2, 128, NTA, 128).transpose(0, 2, 1, 3, 4).reshape(8, 128, 2 * NTA * 128)
    return np.ascontiguousarray(out)


def host_biasB(rpb):
    a = (np.arange(128) // 64)[:, None]
    cp = (np.arange(128) % 64)[:, None]
    b = (np.arange(128) // 64)[None, :]
    c = (np.arange(128) % 64)[None, :]
    qs = np.clip(c - 8, 0, 48)
    colv = (cp >= qs) & (cp < qs + 16)
    dcol = np.clip(cp - c, -15, 15) + 15
    variants = [(-2, "int"), (-1, "int"), (0, "int"), (1, "int"), (2, "int"),
                (2, "all"), (3, "all"), (-3, "all"), (-2, "all")]
    out = np.full((16, 128, NTB, 128), NEG, np.float32)
    for ti, (dl, kind) in enumerate(variants):
        dr = 2 * dl + a - b
        rowv = ((dr >= -4) & (dr <= 3)) if kind == "int" else np.ones_like(dr, bool)
        ri = np.clip(dr + 7, 0, 14)
        valid = rowv & colv
        for h in range(16):
            out[h, :, ti, :] = np.where(valid, rpb[h][ri, dcol], np.float32(NEG))
    out = out.reshape(8, 2, 128, NTB, 128).transpose(0, 2, 1, 3, 4).reshape(8, 128, 2 * NTB * 128)
    return np.ascontiguousarray(out)


def host_w_in(w_in):
    w = w_in.reshape(16, 128, 2, 4, 8, 128)
    w = w.transpose(2, 4, 1, 0, 3, 5)
    return np.ascontiguousarray(w.reshape(16, 128, 16 * 512))


def host_w_out(w_out):
    w = w_out.reshape(16, 128, 2048).transpose(1, 0, 2)
    return np.ascontiguousarray(w.reshape(128, 16 * 2048))


def host_consts():
    c = np.zeros((128, 128 + 192 + 64), np.float32)
    c[:, 320:384] = 1.0
    c[np.arange(128), np.arange(128)] = 1.0
    c[np.arange(64), 128 + 64 + np.arange(64)] = 1.0
    return c


def build(pairs=tuple(range(16)), debug_y=False):
    npairs = len(pairs)
    nc = bass.Bass("TRN2", target_bir_lowering=False)
    x = nc.dram_tensor("x", [S, D], F32, kind="ExternalInput").ap()
    w_in = nc.dram_tensor("w_in", [16, 128, 8192], F32, kind="ExternalInput").ap()
    w_out = nc.dram_tensor("w_out", [128, 16 * 2048], F32, kind="ExternalInput").ap()
    biasA = nc.dram_tensor("biasA", [8, 128, 2 * NTA * 128], F32, kind="ExternalInput").ap()
    biasB = nc.dram_tensor("biasB", [8, 128, 2 * NTB * 128], F32, kind="ExternalInput").ap()
    lng = nc.dram_tensor("lng", [128, D], F32, kind="ExternalInput").ap()
    lnb = nc.dram_tensor("lnb", [128, D], F32, kind="ExternalInput").ap()
    consts = nc.dram_tensor("consts", [128, 384], F32, kind="ExternalInput").ap()
    out = nc.dram_tensor("out", [S, D], F32, kind="ExternalOutput").ap()
    yscr_t = nc.dram_tensor("yscr", [16, 128, 16, 128], BF16,
                            kind=("ExternalOutput" if debug_y else "Internal"))
    yscr = yscr_t.ap()

    sb = nc.alloc_sbuf_tensor

    def alias(name, shape, dtype, base, off=0):
        return nc.alloc_sbuf_tensor_at(name, shape, dtype, offset=nc.lookup_mloc(base).addr + off)

    BIG = sb("BIG", [128, 16, 2048], BF16)
    W = [sb("W%d" % i, [128, 8192], BF16) for i in range(2)]
    ACC = [sb("ACC%d" % i, [128, 2048], F32) for i in range(2)]
    G = [sb("G%d" % i, [128, 2048], BF16) for i in range(2)]
    QT = [sb("QT%d" % i, [128, 2048], BF16) for i in range(2)]
    KT = [sb("KT%d" % i, [128, 2048], BF16) for i in range(2)]
    VT = [sb("VT%d" % i, [128, 2048], BF16) for i in range(2)]
    Vt = [sb("Vt%d" % i, [128, 53, 2, 65], BF16) for i in range(2)]
    biasT = [sb("biasT%d" % i, [128, 2 * NTB * 128], F32) for i in range(2)]
    Sb = [sb("Sb%d" % i, [128, 512], F32) for i in range(2)]
    PT = [sb("PT%d" % i, [128, 512], BF16) for i in range(5)]
    yn = [sb("yn%d" % i, [128, 512], BF16) for i in range(2)]
    ybuf = [sb("ybuf%d" % i, [128, 512], BF16) for i in range(2)]
    cst = sb("cst", [128, 384], BF16)
    hl = sb("hl", [128, 2, 2, 512], BF16)
    ones32 = sb("ones32", [128, 64], F32)
    stt = [sb("stt%d" % i, [128, 24], F32) for i in range(2)]
    mv = [sb("mv%d" % i, [128, 4], F32) for i in range(2)]
    xin = [alias("xin0", [128, 2048], BF16, ACC[0]), alias("xin1", [128, 2048], BF16, ACC[0], 4096),
           alias("xin2", [128, 2048], BF16, ACC[1]), alias("xin3", [128, 2048], BF16, ACC[1], 4096)]
    xres = [alias("xres0", [128, 2048], F32, QT[0]), alias("xres1", [128, 2048], F32, G[0])]
    rbuf = [alias("rbuf0", [128, 2048], F32, KT[0]), alias("rbuf1", [128, 2048], F32, VT[0])]
    yTt = [alias("yTt0", [128, 16, 128], BF16, Vt[0]), alias("yTt1", [128, 16, 128], BF16, Vt[0], 4096)]

    ad = lambda t: nc.lookup_mloc(t).addr
    assert ad(G[1]) == ad(G[0]) + 4096 and ad(QT[1]) == ad(QT[0]) + 4096 and ad(KT[1]) == ad(KT[0]) + 4096 and ad(VT[1]) == ad(VT[0]) + 4096
    P = [nc.alloc_psum_tensor("P%d" % i, [128, 512], F32) for i in range(2)]
    SP = [nc.alloc_psum_tensor("S%d" % i, [128, 512], F32) for i in range(4)]
    OP = [nc.alloc_psum_tensor("O%d" % i, [128, 512], F32) for i in range(2)]
    TP = [P[i][:, :].bitcast(BF16) for i in range(2)]

    ident = cst[:, 0:128]
    E_lo = cst[0:64, 192:320]
    E_sh = cst[0:64, 128:256]

    def AP(t, off, dims):
        return bass.AP(t, off, [list(d) for d in dims])

    def pstride(t):
        return t[:].ap[0][0]

    def col_ap(t, prow, nrow, start, step, n):
        return AP(t, prow * pstride(t) + start, [[pstride(t), nrow], [step, n]])

    sc = Sched()
    cnt = dict(p=0, s=0, o=0, t=0, sb=0, pt=0, yb=0)

    def nxt(k, n):
        v = cnt[k] % n
        cnt[k] += 1
        return v

    sc.dma("pool", lambda e: e.dma_start(out=cst[:], in_=consts), writes=["cst"], sem="cst")
    sc.op("dve", lambda e: e.memset(ones32[:], 1.0), writes=["ones32"])
    for b in range(2):
        sc.op("dve", lambda e, b=b: e.memset(Vt[b][:, :, :, 64:65], 1.0), writes=[("Vt", b)])

    def load_w(p, sl):
        for c4 in range(4):
            sc.dma("pool", lambda e, sl=sl, p=p, c4=c4: e.dma_start(
                out=W[sl][:, 2048 * c4:2048 * c4 + 2048], in_=w_in[p, :, 2048 * c4:2048 * c4 + 2048]),
                writes=[("W", sl)], sem="W%d" % sl)

    grpsA = blocks_A()
    grpsB = blocks_B()
    vtA = vtiles_A()
    vtB = vtiles_B()

    def proj_group(pi, oi, ck):
        p = pairs[pi]
        wsl = pi % 2
        bsl = pi % 2
        pb = nxt("p", 2)
        for dc in range(16):
            sc.op("pe", lambda e, pb=pb, wsl=wsl, dc=dc, oi=oi, ck=ck: e.matmul(
                P[pb][:, :], W[wsl][:, 512 * dc + 128 * oi:512 * dc + 128 * oi + 128],
                BIG[:, dc, 512 * ck:512 * ck + 512], start=(dc == 0), stop=(dc == 15)),
                reads=[("W", wsl), ("xT", 0, ck), ("xT", 1, ck)], writes=[("P", pb)])
        cs = slice(512 * ck, 512 * ck + 512)
        if oi == 0:
            sc.op("act", lambda e, pb=pb, cs=cs: e.copy(QT[bsl][:, cs], P[pb][:, :]),
                  reads=[("P", pb)], writes=[("QT", bsl, ck)])
        elif oi == 1:
            sc.op("act", lambda e, pb=pb, cs=cs: e.copy(KT[bsl][:, cs], P[pb][:, :]),
                  reads=[("P", pb)], writes=[("KT", bsl, ck)])
        elif oi == 2:
            sc.op("dve", lambda e, pb=pb, cs=cs: e.tensor_copy(VT[bsl][:, cs], P[pb][:, :]),
                  reads=[("P", pb)], writes=[("VT", bsl, ck)])
        else:
            sc.op("act", lambda e, pb=pb, cs=cs: e.activation(G[bsl][:, cs], P[pb][:, :], AF.Silu),
                  reads=[("P", pb)], writes=[("G", bsl, ck)])

    def units_P(pi):
        p = pairs[pi]
        isA = p < 8
        bsl = pi % 2
        units = []

        def u_first():
            if isA:
                sc.dma("sp", lambda e: e.dma_start(out=biasT[bsl][:, 0:2 * NTA * 128], in_=biasA[p]),
                       writes=[("bias", bsl)], sem="bias%d" % bsl)
            else:
                sc.dma("sp", lambda e: e.dma_start(out=biasT[bsl][:, 0:2 * NTB * 128], in_=biasB[p - 8]),
                       writes=[("bias", bsl)], sem="bias%d" % bsl)
        units.append(u_first)
        if pi == 0:
            pass
        else:
            for oi in (2, 1, 0, 3):
                for ck in range(4):
                    units.append(lambda oi=oi, ck=ck: proj_group(pi, oi, ck))
        vts = vtA if isA else vtB
        full = [v for v in vts if v[1][2] == 128]
        edge = [v for v in vts if v[1][2] == 64]
        VTk = [("VT", bsl, c) for c in range(4)]
        i0 = 0
        runs = []
        while i0 < len(full):
            run = [full[i0]]
            while len(run) < 8 and i0 + len(run) < len(full) and full[i0 + len(run)][0] == run[-1][0] + 1:
                run.append(full[i0 + len(run)])
            i0 += len(run)
            runs.append(run)

        def u_run(run):
            tb = nxt("p", 2)
            for j, (ti, (st, sp_, nk)) in enumerate(run):
                sc.op("pe", lambda e, tb=tb, j=j, st=st, sp_=sp_: e.transpose(
                    TP[tb][:, 128 * j:128 * j + 128], col_ap(VT[bsl], 0, 128, st, sp_, 128), ident),
                    reads=VTk + ["cst"], writes=[("P", tb)])
            n = len(run)
            t0 = run[0][0]
            dst = Vt[bsl][:, t0:t0 + n, :, 0:64]
            src = TP[tb][:, 0:128 * n].rearrange("p (a h c) -> p a h c", a=n, h=2)
            sc.op("act", lambda e, dst=dst, src=src: e.copy(dst, src),
                  reads=[("P", tb)], writes=[("Vt", bsl)])

        def u_edges(es):
            tb = nxt("p", 2)
            for j, (ti, (st, sp_, nk)) in enumerate(es):
                sc.op("pe", lambda e, tb=tb, j=j, st=st, sp_=sp_: e.transpose(
                    TP[tb][0:64, 128 * j:128 * j + 128], col_ap(VT[bsl], 0, 128, st, sp_, 64), ident),
                    reads=VTk + ["cst"], writes=[("P", tb)])
            for j, (ti, _) in enumerate(es):
                dst = Vt[bsl][0:64, ti, :, 0:64]
                src = TP[tb][0:64, 128 * j:128 * j + 128].rearrange("p (h c) -> p h c", h=2)
                sc.op("act", lambda e, dst=dst, src=src: e.copy(dst, src),
                      reads=[("P", tb)], writes=[("Vt", bsl)])
        for run in runs:
            units.append(lambda run=run: u_run(run))
        for i0 in range(0, len(edge), 8):
            units.append(lambda es=edge[i0:i0 + 8]: u_edges(es))
        if pi + 1 < npairs:
            units.insert(1, lambda: load_w(pairs[pi + 1], (pi + 1) % 2))
        return units

    def units_G(pi):
        return [lambda ck=ck: proj_group(pi, 3, ck) for ck in range(4)]

    def units_T(pi):
        p = pairs[pi]
        isA = p < 8
        bsl = pi % 2
        NT = NTA if isA else NTB
        grps = grpsA if isA else grpsB
        QTk = [("QT", bsl, c) for c in range(4)]
        KTk = [("KT", bsl, c) for c in range(4)]
        units = []
        state = {}
        allA, allB = [], []
        for e_ in range(2):
            prow = 64 * e_
            acc = ACC[e_]
            ps_acc = pstride(acc)
            order = sorted(range(len(grps)), key=lambda gi: (grps[gi]["branch"], grps[gi]["obank"], grps[gi]["ocol"]))
            stream = []
            for gi in order:
                g = grps[gi]
                nb = len(g["blocks"])
                for bi, b in enumerate(g["blocks"]):
                    stream.append(dict(g=g, b=b, first=(bi == 0), last=(bi == nb - 1)))
            packs = []
            cur = []
            for it in stream:
                if it["b"]["k"][2] == 64:
                    if cur:
                        packs.append(cur)
                        cur = []
                    packs.append([it])
                else:
                    cur.append(it)
                    if len(cur) == 4:
                        packs.append(cur)
                        cur = []
            if cur:
                packs.append(cur)

            def flush_obank(key, ob, e_=e_, acc=acc, ps_acc=ps_acc):
                br, obk = key
                src = OP[ob][0:65, :]
                if br in (0, 1):
                    dst = acc[0:65, 512 * obk:512 * obk + 512]
                    sc.op("act", lambda e, dst=dst, src=src: e.copy(dst, src),
                          reads=[("O", ob)], writes=[("ACC", e_)])
                else:
                    if br == 2:
                        view = AP(acc, obk, [[ps_acc, 65], [512, 4], [4, 128]])
                    else:
                        view = AP(acc, 4 * obk, [[ps_acc, 65], [1, 4], [16, 128]])
                    src3 = src.rearrange("p (a b) -> p a b", a=4)
                    sc.op("dve", lambda e, view=view, src3=src3: e.tensor_tensor(view, src3, view, ALU.add),
                          reads=[("O", ob), ("ACC", e_)], writes=[("ACC", e_)])

            def u_packA(pk, e_=e_, prow=prow):
                nk = pk[0]["b"]["k"][2]
                nblk = len(pk)
                sbk = nxt("s", 4)
                for j, it in enumerate(pk):
                    ks, kp, _ = it["b"]["k"]
                    qs, qp = it["g"]["q"]
                    sc.op("pe", lambda e, sbk=sbk, j=j, ks=ks, kp=kp, qs=qs, qp=qp: e.matmul(
                        SP[sbk][0:nk, 128 * j:128 * j + 128],
                        col_ap(KT[bsl], prow, 64, ks, kp, nk), col_ap(QT[bsl], prow, 64, qs, qp, 128),
                        start=True, stop=True),
                        reads=QTk + KTk, writes=[("S", sbk)])
                sbs = nxt("sb", 2)
                bts = [it["b"]["bt"] for it in pk]
                j0 = 0
                while j0 < nblk:
                    j1 = j0 + 1
                    while j1 < nblk and bts[j1] == bts[j1 - 1] + 1:
                        j1 += 1
                    boff = (e_ * NT + bts[j0]) * 128
                    n = 128 * (j1 - j0)
                    sc.op("dve", lambda e, sbs=sbs, sbk=sbk, j0=j0, n=n, boff=boff: e.scalar_tensor_tensor(
                        Sb[sbs][0:nk, 128 * j0:128 * j0 + n], SP[sbk][0:nk, 128 * j0:128 * j0 + n], SCALE,
                        biasT[bsl][0:nk, boff:boff + n], ALU.mult, ALU.add),
                        reads=[("S", sbk), ("bias", bsl)], writes=[("Sb", sbs)])
                    j0 = j1
                pts = nxt("pt", 5)
                sc.op("act", lambda e, pts=pts, sbs=sbs: e.activation(
                    PT[pts][0:nk, 0:128 * nblk], Sb[sbs][0:nk, 0:128 * nblk], AF.Exp),
                    reads=[("Sb", sbs)], writes=[("PT", pts)])
                pk[0]["pts"] = pts

            def u_packB(pk, e_=e_, flush_obank=flush_obank):
                nk = pk[0]["b"]["k"][2]
                pts = pk[0]["pts"]
                for j, it in enumerate(pk):
                    g = it["g"]
                    key = (e_, g["branch"], g["obank"])
                    if key != state.get("cur"):
                        if state.get("cur") is not None:
                            state["flush"](state["cur"][1:], state["ob"])
                        state["cur"] = key
                        state["ob"] = nxt("o", 2)
                        state["flush"] = flush_obank
                    ob = state["ob"]
                    vt = it["b"]["vt"]
                    oc = g["ocol"]
                    sc.op("pe", lambda e, ob=ob, oc=oc, vt=vt, pts=pts, j=j, f=it["first"], l=it["last"]: e.matmul(
                        OP[ob][0:65, oc:oc + 128], Vt[bsl][0:nk, vt, e_, :], PT[pts][0:nk, 128 * j:128 * j + 128],
                        start=f, stop=l),
                        reads=[("Vt", bsl), ("PT", pts)], writes=[("O", ob)])

            for pk in packs:
                pk = [dict(it) for it in pk]
                allA.append(lambda pk=pk, f=u_packA: f(pk))
                allB.append(lambda pk=pk, f=u_packB: f(pk))

        LAG = 4
        for k in range(len(allA)):
            units.append(allA[k])
            if k >= LAG:
                units.append(allB[k - LAG])
        for k in range(max(0, len(allA) - LAG), len(allA)):
            units.append(allB[k])

        def u_flush_last():
            state["flush"](state["cur"][1:], state["ob"])
            state["cur"] = None
        units.append(u_flush_last)

        def fS1(ck):
            cs = slice(512 * ck, 512 * ck + 512)
            for e_ in range(2):
                sc.op("act", lambda e, e_=e_: e.activation(ACC[e_][64:65, cs], ACC[e_][64:65, cs], AF.Ln),
                      reads=[("ACC", e_)], writes=[("ACCd", e_, ck)])
                sc.op("act", lambda e, e_=e_: e.activation(ACC[e_][64:65, cs], ACC[e_][64:65, cs], AF.Exp, scale=-1.0),
                      reads=[("ACCd", e_, ck)], writes=[("ACCd", e_, ck)])
                sc.op("act", lambda e, e_=e_: e.copy(hl[64:65, 0, e_, :], ACC[e_][64:65, cs]),
                      reads=[("ACCd", e_, ck)], writes=[("hl", 0, e_)])
            for e_ in range(2):
                sc.op("pool", lambda e, e_=e_: e.tensor_tensor(
                    hl[64:65, 1, e_, :], ACC[e_][64:65, cs], hl[64:65, 0, e_, :], ALU.subtract),
                    reads=[("ACCd", e_, ck), ("hl", 0, e_)], writes=[("hl", 1, e_)])

        def fS2(ck):
            cs = slice(512 * ck, 512 * ck + 512)
            pbs = []
            for e_ in range(2):
                pb = nxt("p", 2)
                pbs.append(pb)
                sc.op("pe", lambda e, pb=pb, e_=e_: e.matmul(
                    P[pb][0:64, :], cst[64:65, 320:384], hl[64:65, 0, e_, :], start=True, stop=False),
                    reads=["cst", ("hl", 0, e_)], writes=[("P", pb)])
                sc.op("pe", lambda e, pb=pb, e_=e_: e.matmul(
                    P[pb][0:64, :], cst[64:65, 320:384], hl[64:65, 1, e_, :], start=False, stop=True),
                    reads=["cst", ("hl", 1, e_)], writes=[("P", pb)])
            for e_ in range(2):
                pb = pbs[e_]
                sc.op("dve", lambda e, pb=pb, e_=e_: e.tensor_tensor(
                    yn[e_][0:64, :], ACC[e_][0:64, cs], P[pb][0:64, :], ALU.mult),
                    reads=[("P", pb), ("ACC", e_)], writes=[("yn", e_)])

        def fS3(ck):
            cs = slice(512 * ck, 512 * ck + 512)
            pb = nxt("p", 2)
            sc.op("pe", lambda e, pb=pb: e.matmul(P[pb][:, :], E_lo, yn[0][0:64, :], start=True, stop=False),
                  reads=["cst", ("yn", 0)], writes=[("P", pb)])
            sc.op("pe", lambda e, pb=pb: e.matmul(P[pb][:, :], E_sh, yn[1][0:64, :], start=False, stop=True),
                  reads=["cst", ("yn", 1)], writes=[("P", pb)])
            ys = nxt("yb", 2)
            sc.op("dve", lambda e, pb=pb, ys=ys: e.tensor_tensor(
                ybuf[ys][:, :], P[pb][:, :], G[bsl][:, cs], ALU.mult),
                reads=[("P", pb), ("G", bsl, ck)], writes=[("ybuf", ys)])
            dst = yscr[4 * ck:4 * ck + 4, :, p, :].rearrange("i e t -> e i t")
            src = ybuf[ys][:, :].rearrange("e (i t) -> e i t", i=4)
            sc.dma("sp", lambda e, dst=dst, src=src: e.dma_start(out=dst, in_=src),
                   reads=[("ybuf", ys)], writes=["yscr"], sem="yst%d" % ys)

        seq = [("1", 0), ("2", 0), ("1", 1), ("3", 0), ("2", 1), ("1", 2), ("3", 1), ("2", 2), ("1", 3),
               ("3", 2), ("2", 3), ("3", 3)]
        fmap = {"1": fS1, "2": fS2, "3": fS3}
        for kind, ck in seq:
            units.append(lambda kind=kind, ck=ck: fmap[kind](ck))
        return units

    def phase0():
        for i in range(16):
            sl = i % 4
            sc.dma("pool", lambda e, sl=sl, i=i: e.dma_start(out=xin[sl][:], in_=x[128 * i:128 * i + 128, :]),
                   writes=[("xin", sl)], sem="xin%d" % sl)
            if i == 1 and npairs > 0:
                load_w(pairs[0], 0)
            for g in range(2):
                tb = nxt("p", 2)
                for j in range(8):
                    dc = 8 * g + j
                    sc.op("pe", lambda e, tb=tb, j=j, sl=sl, dc=dc: e.transpose(
                        TP[tb][:, 128 * j:128 * j + 128], xin[sl][:, 128 * dc:128 * dc + 128], ident),
                        reads=[("xin", sl), "cst"], writes=[("P", tb)])
                dst = BIG[:, 8 * g:8 * g + 8, 128 * i:128 * i + 128]
                src = TP[tb][:, :].rearrange("p (a b) -> p a b", a=8)
                if g == 0:
                    sc.op("act", lambda e, dst=dst, src=src: e.copy(dst, src),
                          reads=[("P", tb)], writes=[("xT", g, i // 4)])
                else:
                    sc.op("dve", lambda e, dst=dst, src=src: e.tensor_copy(dst, src),
                          reads=[("P", tb)], writes=[("xT", g, i // 4)])
            if i % 4 == 3 and npairs > 0:
                for oi in (2, 1, 0, 3):
                    proj_group(0, oi, i // 4)


    phase0()
    if npairs > 0:
        for u in units_P(0):
            u()
    for pi in range(npairs):
        tu = units_T(pi)
        pu = units_P(pi + 1) if pi + 1 < npairs else []
        nT, nP = len(tu), len(pu)
        span = max(1, int(nT * 0.85))
        ip = 0
        for it_, u in enumerate(tu):
            u()
            while ip < nP and (ip + 1) * span <= (it_ + 1) * nP:
                pu[ip]()
                ip += 1
        while ip < nP:
            pu[ip]()
            ip += 1
        if pi + 1 == npairs - 1 or npairs == 1:
            for c8 in range(8):
                sc.dma("pool", lambda e, c8=c8: e.dma_start(
                    out=BIG[:, 2 * c8:2 * c8 + 2, :],
                    in_=w_out[:, 4096 * c8:4096 * c8 + 4096].rearrange("p (a b) -> p a b", a=2)),
                    writes=[("WO", c8)] + [("xT", g_, c_) for g_ in range(2) for c_ in range(4)], sem="WO%d" % c8)

    sc.barrier()
    WO = BIG
    WOk = [("WO", c8) for c8 in range(8)]
    gain = ACC[0]
    lbias = ACC[1]
    sc.dma("sp", lambda e: e.dma_start(out=gain[:], in_=lng), writes=["gain"], sem="gain")
    sc.dma("sp", lambda e: e.dma_start(out=lbias[:], in_=lnb), writes=["lbias"], sem="lbias")
    def p2_loads(i):
        ysl = i % 2
        xsl = i % 2
        sc.dma("sp", lambda e, ysl=ysl, i=i: e.dma_start(out=yTt[ysl][:], in_=yscr[i]),
               reads=["yscr"], writes=[("yTt", ysl)], sem="yTt%d" % ysl)
        sc.dma("sp", lambda e, i=i, xsl=xsl: e.dma_start(out=xres[xsl][:], in_=x[128 * i:128 * i + 128, :]),
               writes=[("xres", xsl)], sem="xres%d" % xsl)

    p2_loads(0)
    pend_tail = []
    for i in range(16):
        ysl = i % 2
        rsl = i % 2
        xsl = i % 2
        if i + 1 < 16:
            p2_loads(i + 1)
        for n in range(4):
            cs = slice(512 * n, 512 * n + 512)
            pb = nxt("p", 2)
            for ec in range(16):
                sc.op("pe", lambda e, pb=pb, ysl=ysl, ec=ec, cs=cs: e.matmul(
                    P[pb][:, :], yTt[ysl][:, ec, :], WO[:, ec, cs], start=(ec == 0), stop=(ec == 15)),
                    reads=[("yTt", ysl), ("WO", ec // 2)], writes=[("P", pb)])
            sc.op("dve", lambda e, pb=pb, rsl=rsl, cs=cs, xsl=xsl: e.scalar_tensor_tensor(
                rbuf[rsl][:, cs], xres[xsl][:, cs], ALPHA, P[pb][:, :], ALU.mult, ALU.add),
                reads=[("P", pb), ("xres", xsl)], writes=[("rbuf", rsl, n)])
            sc.op("dve", lambda e, rsl=rsl, n=n, cs=cs: e.bn_stats(stt[rsl][:, 6 * n:6 * n + 6], rbuf[rsl][:, cs]),
                  reads=[("rbuf", rsl, n)], writes=[("stt", rsl, n)])
            if pend_tail:
                pend_tail.pop(0)()
                if n == 3:
                    while pend_tail:
                        pend_tail.pop(0)()
        rk = [("rbuf", rsl, n) for n in range(4)]
        sc.op("dve", lambda e, rsl=rsl: e.bn_aggr(mv[rsl][:, 0:2], stt[rsl][:, 0:24]),
              reads=[("stt", rsl, n) for n in range(4)], writes=[("mv", rsl)])
        sc.op("dve", lambda e, rsl=rsl: e.tensor_scalar(
            mv[rsl][:, 2:3], mv[rsl][:, 1:2], LN_EPS, None, ALU.add),
            reads=[("mv", rsl)], writes=[("mv2", rsl)])
        sc.op("act", lambda e, rsl=rsl: e.sqrt(mv[rsl][:, 2:3], mv[rsl][:, 2:3]),
              reads=[("mv2", rsl)], writes=[("mv2", rsl)])
        sc.op("dve", lambda e, rsl=rsl: e.reciprocal(mv[rsl][:, 2:3], mv[rsl][:, 2:3]),
              reads=[("mv2", rsl)], writes=[("mv2", rsl)])
        sc.op("dve", lambda e, rsl=rsl: e.scalar_tensor_tensor(
            mv[rsl][:, 3:4], mv[rsl][:, 0:1], -1.0, mv[rsl][:, 2:3], ALU.mult, ALU.mult),
            reads=[("mv", rsl), ("mv2", rsl)], writes=[("mv3", rsl)])
        sc.op("act", lambda e, rsl=rsl: e.activation(
            rbuf[rsl][:, :], rbuf[rsl][:, :], AF.Identity, bias=mv[rsl][:, 3:4], scale=mv[rsl][:, 2:3]),
            reads=rk + [("mv2", rsl), ("mv3", rsl)], writes=rk)

        def mk_tail(rsl=rsl, i=i, rk=rk):
            ops = []
            for hh in range(2):
                hs = slice(1024 * hh, 1024 * hh + 1024)
                ops.append(lambda hs=hs, hh=hh: sc.op("dve", lambda e: e.tensor_tensor(
                    rbuf[rsl][:, hs], rbuf[rsl][:, hs], gain[:, hs], ALU.mult),
                    reads=rk + ["gain"], writes=[("rbh", rsl, hh)]))
                ops.append(lambda hs=hs, hh=hh: sc.op("dve", lambda e: e.tensor_tensor(
                    rbuf[rsl][:, hs], rbuf[rsl][:, hs], lbias[:, hs], ALU.add),
                    reads=[("rbh", rsl, hh), "lbias"], writes=[("rbh", rsl, hh)]))
            ops.append(lambda: sc.dma("sp", lambda e: e.dma_start(out=out[128 * i:128 * i + 128, :], in_=rbuf[rsl][:]),
                                      reads=[("rbh", rsl, 0), ("rbh", rsl, 1)], writes=rk + [("out", i)], sem="ost%d" % rsl))
            return ops
        pend_tail = mk_tail()
    for op_ in pend_tail:
        op_()
    sc.op("sp", None, reads=[("out", i) for i in range(16)] + (["yscr"] if debug_y else []))
    sc.emit(nc)
    return nc


_CACHE = {}


def prep_inputs(x, w_in, w_out, t5_bias, na_rpb, ln_gain, ln_bias):
    x = np.asarray(x, np.float32)
    shared = dict(
        w_in=host_w_in(np.asarray(w_in, np.float32)[0]),
        w_out=host_w_out(np.asarray(w_out, np.float32)[0]),
        biasA=host_biasA(np.asarray(t5_bias, np.float32)),
        biasB=host_biasB(np.asarray(na_rpb, np.float32)[0]),
        lng=np.ascontiguousarray(np.broadcast_to(np.asarray(ln_gain, np.float32)[0][None, :], (128, D))),
        lnb=np.ascontiguousarray(np.broadcast_to(np.asarray(ln_bias, np.float32)[0][None, :], (128, D))),
        consts=host_consts(),
    )
    in_maps = []
    for b in range(8):
        m = dict(shared)
        m["x"] = np.ascontiguousarray(x[b])
        in_maps.append(m)
    return in_maps


def kernel(x, w_in, w_out, t5_bias, na_rpb, ln_gain, ln_bias):
    in_maps = prep_inputs(x, w_in, w_out, t5_bias, na_rpb, ln_gain, ln_bias)
    if "nc" not in _CACHE:
        _CACHE["nc"] = build()
    nc = _CACHE["nc"]
    res = run_bass_kernel_spmd(nc, in_maps, core_ids=list(range(8)))
    outs = [np.asarray(r["out"], np.float32) for r in res.results]
    return np.stack(outs, axis=0)
```
